# Optimizing a Trainium2 kernel written in Bass

```python
import jax, jax.numpy as jnp
from jax import lax
import numpy as np

D_MODEL = 1024
BATCH = 4
SEQ = 8192
DEPTH = 4

N_A_LAYERS = DEPTH // 2
N_B_LAYERS = DEPTH - N_A_LAYERS

LRU_WIDTH = D_MODEL * 3 // 2
LRU_BLOCKS = 16
LRU_BLOCK_W = LRU_WIDTH // LRU_BLOCKS
CONV_WIDTH = 4
LRU_C = 8.0

N_Q_HEADS = 16
N_KV_HEADS = 2
GROUP = N_Q_HEADS // N_KV_HEADS
HEAD_DIM = 64
ATT_WIDTH = N_Q_HEADS * HEAD_DIM
WINDOW = 128
BLOCK = 128

EPS = 1e-6

kernel_name = "yoco_rglru_swa_sink_alibi_trunk"


def rmsnorm(x, g):
    x32 = x.astype(jnp.float32)
    y = x32 * lax.rsqrt(jnp.mean(x32 * x32, axis=-1, keepdims=True) + EPS)
    return (y * g.astype(jnp.float32)).astype(x.dtype)


def causal_depthwise_conv(u, w, b):
    r = u.shape[-1]
    out = lax.conv_general_dilated(
        u, w.astype(u.dtype)[:, None, :], window_strides=(1,),
        padding=[(CONV_WIDTH - 1, 0)],
        dimension_numbers=("NWC", "WIO", "NWC"),
        feature_group_count=r)
    return out + b.astype(u.dtype)


def _linear_combine(c1, c2):
    a1, b1 = c1
    a2, b2 = c2
    return a1 * a2, a2 * b1 + b2


def rglru_layer(x, norm_g, w_in, conv_w, conv_b, wr, br, wi, bi, lam, w_out):
    bsz, seq, _ = x.shape
    h = rmsnorm(x, norm_g)
    u = h @ w_in
    xb, gate = u[..., :LRU_WIDTH], u[..., LRU_WIDTH:]
    xb = causal_depthwise_conv(xb, conv_w, conv_b)
    xs = xb.reshape(bsz, seq, LRU_BLOCKS, LRU_BLOCK_W)
    r = jax.nn.sigmoid((jnp.einsum('bshi,hij->bshj', xs, wr).reshape(bsz, seq, LRU_WIDTH) + br).astype(jnp.float32))
    i = jax.nn.sigmoid((jnp.einsum('bshi,hij->bshj', xs, wi).reshape(bsz, seq, LRU_WIDTH) + bi).astype(jnp.float32))
    log_a = -LRU_C * r * jax.nn.softplus(-lam.astype(jnp.float32))
    a = jnp.exp(log_a)
    b = jnp.sqrt(-jnp.expm1(2.0 * log_a)) * (i * xb.astype(jnp.float32))
    _, hs = lax.associative_scan(_linear_combine, (a, b), axis=1)
    y = hs.astype(x.dtype) * jax.nn.silu(gate)
    return x + y @ w_out


def sliding_window_sink_attention(q, k, v, sinks):
    bsz, seq = q.shape[0], q.shape[1]
    nb = seq // BLOCK
    qb = q.reshape(bsz, nb, BLOCK, N_KV_HEADS, GROUP, HEAD_DIM)

    def band(t):
        cur = t.reshape(bsz, nb, BLOCK, N_KV_HEADS, HEAD_DIM)
        prev = jnp.concatenate([jnp.zeros_like(cur[:, :1]), cur[:, :-1]], axis=1)
        return jnp.concatenate([prev, cur], axis=2)

    kb, vb = band(k), band(v)
    scale = HEAD_DIM ** -0.5
    s = jnp.einsum('bnqkgd,bnskd->bnkgqs', qb, kb).astype(jnp.float32) * scale

    qi = jnp.arange(BLOCK)[:, None] + BLOCK
    kj = jnp.arange(2 * BLOCK)[None, :]
    dist = qi - kj
    abs_k = jnp.arange(nb)[:, None, None] * BLOCK - BLOCK + kj[None]
    valid = (dist >= 0)[None] & (dist < WINDOW)[None] & (abs_k >= 0)

    slopes = jnp.exp2(-8.0 * jnp.arange(1, N_Q_HEADS + 1, dtype=jnp.float32) / N_Q_HEADS)
    slopes = slopes.reshape(N_KV_HEADS, GROUP)
    alibi = -slopes[:, :, None, None] * dist.astype(jnp.float32)[None, None]

    s = jnp.where(valid[None, :, None, None], s + alibi[None, None], -jnp.inf)
    sink = sinks.astype(jnp.float32).reshape(N_KV_HEADS, GROUP)[None, None, :, :, None, None]
    m = jnp.maximum(jnp.max(s, axis=-1, keepdims=True), sink)
    p = jnp.exp(s - m)
    denom = jnp.sum(p, axis=-1, keepdims=True) + jnp.exp(sink - m)
    p = (p / denom).astype(v.dtype)
    o = jnp.einsum('bnkgqs,bnskd->bnqkgd', p, vb)
    return o.reshape(bsz, seq, ATT_WIDTH)


def swa_layer(x, k, v, norm_g, w_in, q_norm_g, sinks, w_out):
    bsz, seq, _ = x.shape
    h = rmsnorm(x, norm_g)
    u = h @ w_in
    q, gate = u[..., :ATT_WIDTH], u[..., ATT_WIDTH:]
    q = rmsnorm(q.reshape(bsz, seq, N_Q_HEADS, HEAD_DIM), q_norm_g)
    o = sliding_window_sink_attention(q, k, v, sinks)
    y = o * jax.nn.silu(gate)
    return x + y @ w_out


def setup_inputs(seed: int = 0) -> dict:
    key = jax.random.key(seed)
    ks = jax.random.split(key, 20)
    nA, nB, R, D = N_A_LAYERS, N_B_LAYERS, LRU_WIDTH, D_MODEL
    f32 = jnp.float32
    nrm = lambda k, shape, s: jax.random.normal(k, shape, f32) * s
    x = jax.random.normal(ks[0], (BATCH, SEQ, D), f32)
    a_norm_g = 1.0 + nrm(ks[1], (nA, D), 0.02)
    a_w_in = nrm(ks[2], (nA, D, 2 * R), D ** -0.5)
    a_conv_w = nrm(ks[3], (nA, CONV_WIDTH, R), CONV_WIDTH ** -0.5)
    a_conv_b = nrm(ks[4], (nA, R), 0.01)
    a_gate_r_w = nrm(ks[5], (nA, LRU_BLOCKS, LRU_BLOCK_W, LRU_BLOCK_W), LRU_BLOCK_W ** -0.5)
    a_gate_r_b = nrm(ks[6], (nA, R), 0.01)
    a_gate_i_w = nrm(ks[7], (nA, LRU_BLOCKS, LRU_BLOCK_W, LRU_BLOCK_W), LRU_BLOCK_W ** -0.5)
    a_gate_i_b = nrm(ks[8], (nA, R), 0.01)
    a8 = jax.random.uniform(ks[9], (nA, R), f32, 0.9, 0.999)
    a0 = a8 ** (1.0 / LRU_C)
    a_lambda = jnp.log(a0) - jnp.log1p(-a0)
    a_w_out = nrm(ks[10], (nA, R, D), R ** -0.5)
    kv_norm_g = 1.0 + nrm(ks[11], (D,), 0.02)
    w_kv = nrm(ks[12], (D, 2 * N_KV_HEADS * HEAD_DIM), D ** -0.5)
    k_norm_g = 1.0 + nrm(ks[13], (HEAD_DIM,), 0.02)
    b_norm_g = 1.0 + nrm(ks[14], (nB, D), 0.02)
    b_w_in = nrm(ks[15], (nB, D, 2 * ATT_WIDTH), D ** -0.5)
    q_norm_g = 1.0 + nrm(ks[16], (nB, HEAD_DIM), 0.02)
    sinks = nrm(ks[17], (nB, N_Q_HEADS), 0.5)
    b_w_out = nrm(ks[18], (nB, ATT_WIDTH, D), ATT_WIDTH ** -0.5)
    return {"x": x, "a_norm_g": a_norm_g, "a_w_in": a_w_in, "a_conv_w": a_conv_w,
            "a_conv_b": a_conv_b, "a_gate_r_w": a_gate_r_w, "a_gate_r_b": a_gate_r_b,
            "a_gate_i_w": a_gate_i_w, "a_gate_i_b": a_gate_i_b, "a_lambda": a_lambda,
            "a_w_out": a_w_out, "kv_norm_g": kv_norm_g, "w_kv": w_kv, "k_norm_g": k_norm_g,
            "b_norm_g": b_norm_g, "b_w_in": b_w_in, "q_norm_g": q_norm_g, "sinks": sinks,
            "b_w_out": b_w_out}


def reference(x, a_norm_g, a_w_in, a_conv_w, a_conv_b, a_gate_r_w, a_gate_r_b,
              a_gate_i_w, a_gate_i_b, a_lambda, a_w_out, kv_norm_g, w_kv, k_norm_g,
              b_norm_g, b_w_in, q_norm_g, sinks, b_w_out):
    bsz, seq, _ = x.shape
    k = v = None
    for layer in range(DEPTH):
        if layer < N_A_LAYERS:
            l = layer
            x = rglru_layer(x, a_norm_g[l], a_w_in[l], a_conv_w[l], a_conv_b[l],
                            a_gate_r_w[l], a_gate_r_b[l], a_gate_i_w[l], a_gate_i_b[l],
                            a_lambda[l], a_w_out[l])
            if layer == N_A_LAYERS - 1:
                kv = rmsnorm(x, kv_norm_g) @ w_kv
                kv = kv.reshape(bsz, seq, 2, N_KV_HEADS, HEAD_DIM)
                k = rmsnorm(kv[:, :, 0], k_norm_g)
                v = kv[:, :, 1]
        else:
            l = layer - N_A_LAYERS
            x = swa_layer(x, k, v, b_norm_g[l], b_w_in[l], q_norm_g[l], sinks[l], b_w_out[l])
    return x
```

```python
import numpy as np
from contextlib import ExitStack
import concourse.bass as bass
import concourse.mybir as mybir
from concourse.bass_utils import run_bass_kernel_spmd

F32 = mybir.dt.float32
BF16 = mybir.dt.bfloat16
ALU = mybir.AluOpType
AF = mybir.ActivationFunctionType
AX = mybir.AxisListType

COMPUTE = ("pe", "act", "dve", "pool")
EPS = 1e-6


class Op:
    __slots__ = ("eng", "fn", "deps", "is_dma", "dma_key", "sig", "sem", "cnt", "order", "wkey")

    def __init__(self, eng, fn, is_dma=False, dma_key=None):
        self.eng = eng
        self.fn = fn
        self.deps = []
        self.is_dma = is_dma
        self.dma_key = dma_key
        self.sig = False
        self.sem = None
        self.cnt = 0
        self.order = 0
        self.wkey = None


class Prog:
    def __init__(self):
        self.queues = {"pe": [], "act": [], "dve": [], "pool": [], "sp": []}
        self.last_w = {}
        self.readers = {}
        self.all_ops = []
        self.barrier_deps = {}
        self.dma_last = {}

    def add(self, eng, fn, reads=(), writes=(), is_dma=False, dma_key=None):
        op = Op(eng, fn, is_dma, dma_key)
        op.order = len(self.all_ops)
        op.wkey = tuple(writes)
        deps = []
        for r in reads:
            w = self.last_w.get(r)
            if w is not None:
                deps.append(w)
        for r in writes:
            w = self.last_w.get(r)
            if w is not None:
                deps.append(w)
            deps.extend(self.readers.get(r, ()))
        if eng in self.barrier_deps:
            deps.extend(self.barrier_deps.pop(eng))
        seen = set()
        for d in deps:
            if id(d) in seen:
                continue
            seen.add(id(d))
            if d.eng == eng and eng == "pe" and not d.is_dma:
                continue
            op.deps.append(d)
        for r in writes:
            self.last_w[r] = op
            self.readers[r] = []
        for r in reads:
            if r not in writes:
                self.readers.setdefault(r, []).append(op)
        self.queues[eng].append(op)
        self.all_ops.append(op)
        if is_dma:
            self.dma_last[dma_key] = op
        return op

    def barrier(self):
        lasts = [q[-1] for q in self.queues.values() if q]
        lasts.extend(self.dma_last.values())
        for e in self.queues:
            self.barrier_deps[e] = list(lasts)

    def compress_pe_deps(self):
        q = self.queues["pe"]
        nxt = [None] * len(q)
        anchor = None
        for i in range(len(q) - 1, -1, -1):
            if i == len(q) - 1 or q[i + 1].wkey != q[i].wkey:
                anchor = q[i]
            nxt[i] = anchor
        pos = {id(op): i for i, op in enumerate(q)}
        for op in self.all_ops:
            new = []
            seen = set()
            for d in op.deps:
                if d.eng == "pe" and not d.is_dma:
                    a = nxt[pos[id(d)]]
                    if a is not d and a.order < op.order:
                        d = a
                if id(d) not in seen:
                    seen.add(id(d))
                    new.append(d)
            op.deps = new

    def emit(self, nc, stack, final_waits=()):
        self.compress_pe_deps()
        for op in self.all_ops:
            for d in op.deps:
                d.sig = True
        eng_sem = {e: stack.enter_context(nc.semaphore(f"s_{e}")) for e in COMPUTE}
        dma_sems, dma_cnt = {}, {}
        eng_cnt = {e: 0 for e in COMPUTE}
        for e, q in self.queues.items():
            for op in q:
                if op.is_dma:
                    k = op.dma_key
                    if k not in dma_sems:
                        dma_sems[k] = stack.enter_context(nc.semaphore(f"d_{k}"))
                        dma_cnt[k] = 0
                    dma_cnt[k] += 16
                    op.sem, op.cnt = dma_sems[k], dma_cnt[k]
                elif op.sig:
                    eng_cnt[e] += 1
                    op.sem, op.cnt = eng_sem[e], eng_cnt[e]
        block = stack.enter_context(nc.Block())

        def run_queue(engine, q):
            waited = {}
            for op in q:
                for d in op.deps:
                    key = id(d.sem)
                    if waited.get(key, 0) >= d.cnt:
                        continue
                    engine.wait_ge(d.sem, d.cnt)
                    waited[key] = d.cnt
                ins = op.fn(engine)
                if op.is_dma:
                    ins.then_inc(op.sem, 16)
                elif op.sig:
                    ins.then_inc(op.sem, 1)
            return waited

        qs = self.queues

        @block.tensor
        def _(eng):
            run_queue(eng, qs["pe"])

        @block.scalar
        def _(eng):
            run_queue(eng, qs["act"])

        @block.vector
        def _(eng):
            run_queue(eng, qs["dve"])

        @block.gpsimd
        def _(eng):
            run_queue(eng, qs["pool"])

        @block.sync
        def _(eng):
            waited = run_queue(eng, qs["sp"])
            for op in final_waits:
                if waited.get(id(op.sem), 0) < op.cnt:
                    eng.wait_ge(op.sem, op.cnt)
                    waited[id(op.sem)] = op.cnt


GATE_COMBOS = [(0, 0), (1, 0), (0, 1), (1, 1), (2, 1), (1, 2), (2, 2)]

DBG = {}


def build(NT, phases=("A0", "A1", "KV", "B0", "B1")):
    S = NT * 512
    NH = NT // 2
    NBL = NH * 4 + 4
    SL = NBL * 128
    nc = bass.Bass("TRN2", target_bir_lowering=False)

    def din(name, shape):
        return nc.dram_tensor(name, list(shape), F32, kind="ExternalInput").ap()

    x_in = din("x", [S, 1024])
    ident_d = din("ident", [128, 128])
    a_w_in = [din(f"a_w_in{l}", [1024, 3072]) for l in range(2)]
    a_w_out = [din(f"a_w_out{l}", [1536, 1024]) for l in range(2)]
    a_ng = [din(f"a_ng{l}", [128, 8]) for l in range(2)]
    a_vec = [din(f"a_vec{l}", [128, 96]) for l in range(2)]
    a_gw = [din(f"a_gw{l}", [128, 7168]) for l in range(2)]
    w_kv = din("w_kv", [1024, 256])
    kv_ng = din("kv_ng", [128, 8])
    gk_b = din("gk_b", [128, 64])
    b_w_in = [din(f"b_w_in{l}", [1024, 2048]) for l in range(2)]
    b_w_out = [din(f"b_w_out{l}", [1024, 1024]) for l in range(2)]
    b_ng = [din(f"b_ng{l}", [128, 8]) for l in range(2)]
    b_gq = [din(f"b_gq{l}", [128, 1]) for l in range(2)]
    b_sink = [din(f"b_sink{l}", [128, 16]) for l in range(2)]
    dmat = din("dmat", [128, 4096])
    sel_d = din("sel", [128, 2])
    hmask_d = din("hmask", [128, 1])
    y_out = nc.dram_tensor("y", [NH * 512, 1024], F32, kind="ExternalOutput").ap()
    xs = nc.dram_tensor("xs", [S, 1024], F32, kind="Internal").ap()
    xsel = nc.dram_tensor("xsel", [NH * 512, 1024], F32, kind="Internal").ap()
    xs2 = nc.dram_tensor("xs2", [NH * 512, 1024], F32, kind="Internal").ap()

    P = Prog()
    out_ops = {}

    with ExitStack() as st:
        ARENA = 53200
        arena = st.enter_context(nc.sbuf_tensor("arena", [128, ARENA], F32))
        off = [0]

        def carve(n_words, dtype=F32, shape=None):
            assert off[0] + n_words <= ARENA, (off[0], n_words)
            a = arena[:, off[0]:off[0] + n_words]
            off[0] += n_words
            if dtype != F32:
                a = a.bitcast(dtype)
            if shape is not None:
                names = " ".join(f"d{i}" for i in range(len(shape)))
                kw = {f"d{i}": s for i, s in enumerate(shape)}
                a = a.rearrange(f"p ({names}) -> p {names}", **kw)
            return a

        TP = st.enter_context(nc.psum_tensor("TP", [128, 1024], BF16))[:, :]
        U = [st.enter_context(nc.psum_tensor(f"U{i}", [128, 512], F32))[:, :] for i in range(3)]
        G = [st.enter_context(nc.psum_tensor(f"G{i}", [128, 512], F32))[:, :] for i in range(2)]
        O = [st.enter_context(nc.psum_tensor(f"O{i}", [128, 512], F32))[:, :] for i in range(2)]

        ident = carve(64, BF16)
        identf = carve(128)
        xt = [carve(4096, F32, (4, 1024)) for _ in range(2)]
        junk = carve(512, BF16)
        ss = [carve(4) for _ in range(2)]
        ms = [carve(4) for _ in range(2)]
        sd = [carve(4) for _ in range(2)]
        rstd = [carve(4) for _ in range(2)]
        xn = carve(1024, BF16, (2, 1024))
        xnT = [carve(2048, BF16, (8, 512)) for _ in range(2)]
        common_end = off[0]

        P.add("sp", lambda e: e.dma_start(out=identf, in_=ident_d), writes=["identf"], is_dma=True, dma_key="ident")
        P.add("dve", lambda e: e.tensor_copy(ident, identf), reads=["identf"], writes=["ident"])

        def load_x(src, src_name, t, slot=None):
            slot = t % 2 if slot is None else slot
            xts = xt[slot]
            srcv = src[t * 512:(t + 1) * 512, :].rearrange("(j p) d -> p j d", p=128)
            P.add("sp", lambda e: e.dma_start(out=xts, in_=srcv), reads=[(src_name, t)], writes=[f"xt{slot}"],
                  is_dma=True, dma_key=f"xt{slot}")

        def frontA(src, src_name, t):
            slot = t % 2
            xts = xt[slot]
            for j in range(4):
                P.add("act", lambda e, j=j: e.activation(out=junk, in_=xts[:, j, :], func=AF.Square,
                                                        accum_out=ss[slot][:, j:j + 1]),
                      reads=[f"xt{slot}"], writes=["junk", f"ss{slot}"])
            P.add("dve", lambda e: e.tensor_scalar(out=ms[slot], in0=ss[slot], scalar1=1.0 / 1024, scalar2=EPS,
                                                   op0=ALU.mult, op1=ALU.add), reads=[f"ss{slot}"], writes=[f"ms{slot}"])

        def frontB(t, use_ln=False):
            slot = t % 2
            xts = xt[slot]
            xT = xnT[slot]
            if use_ln:
                P.add("act", lambda e: e.activation(out=sd[slot], in_=ms[slot], func=AF.Ln), reads=[f"ms{slot}"], writes=[f"sd{slot}"])
                P.add("act", lambda e: e.activation(out=rstd[slot], in_=sd[slot], func=AF.Exp, scale=-0.5), reads=[f"sd{slot}"], writes=[f"rstd{slot}"])
            else:
                P.add("act", lambda e: e.activation(out=sd[slot], in_=ms[slot], func=AF.Sqrt), reads=[f"ms{slot}"], writes=[f"sd{slot}"])
                P.add("dve", lambda e: e.reciprocal(out=rstd[slot], in_=sd[slot]), reads=[f"sd{slot}"], writes=[f"rstd{slot}"])
            for j in range(4):
                P.add("pool", lambda e, j=j: e.tensor_scalar(out=xn[:, j % 2, :], in0=xts[:, j, :], scalar1=rstd[slot][:, j:j + 1],
                                                             scalar2=1.0, op0=ALU.mult, op1=ALU.mult),
                      reads=[f"xt{slot}", f"rstd{slot}"], writes=[f"xn{j % 2}"])
                for k in range(8):
                    P.add("pe", lambda e, j=j, k=k: e.transpose(out=TP[:, k * 128:(k + 1) * 128], in_=xn[:, j % 2, k * 128:(k + 1) * 128],
                                                               identity=ident), reads=[f"xn{j % 2}", "ident"], writes=["TP"])
                P.add("dve", lambda e, j=j: e.tensor_copy(xT[:, :, j * 128:(j + 1) * 128], TP.rearrange("p (k n) -> p k n", k=8)),
                      reads=["TP"], writes=[f"xnT{slot}_{j}"])

        def front(src, src_name, t, use_ln=False):
            frontA(src, src_name, t)
            frontB(t, use_ln)

        def store_x(dst, dst_name, t):
            slot = t % 2
            dstv = dst[t * 512:(t + 1) * 512, :].rearrange("(j p) d -> p j d", p=128)
            o = P.add("sp", lambda e: e.dma_start(out=dstv, in_=xt[slot]), reads=[f"xt{slot}"],
                      writes=[(dst_name, t)], is_dma=True, dma_key=f"st{slot}")
            out_ops[f"st{slot}"] = o

        def layer_A(l, src, src_name, dst, dst_name):
            P.barrier()
            off[0] = common_end
            Win = carve(12288, BF16, (8, 3072))
            Wout = carve(6144, BF16, (12, 1024))
            gw = carve(3584, BF16, (2, 28, 128))
            vec = carve(96)
            ngt = carve(8)
            tsp = carve(12)
            hcr = carve(12)
            hcr2 = carve(12)
            hbr = carve(12)
            hbi = carve(12)
            hstate = carve(12)
            halo = carve(36, F32, (12, 3))
            _o = off[0]
            wst = [carve(3072), carve(3072)]
            off[0] = _o
            xc32 = [carve(1536, F32, (3, 512)) for _ in range(2)]
            xcb = [carve(768, BF16, (3, 512)) for _ in range(2)]
            trbs = [carve(1536, F32, (3, 512)) for _ in range(2)]
            tibs = [carve(1536, F32, (3, 512)) for _ in range(2)]
            qb = carve(1536, F32, (3, 512))
            spb = carve(1536, F32, (3, 512))
            ybf = carve(3072, BF16, (12, 512))

            P.add("sp", lambda e: e.dma_start(out=vec, in_=a_vec[l]), writes=["prm"], is_dma=True, dma_key="prm")
            P.add("sp", lambda e: e.dma_start(out=ngt, in_=a_ng[l]), writes=["prm2"], is_dma=True, dma_key="prm2")
            lam = vec[:, 84:96]
            P.add("act", lambda e: e.activation(out=tsp, in_=lam, func=AF.Exp, scale=-1.0), reads=["prm"], writes=["tsp"])
            P.add("act", lambda e: e.activation(out=tsp, in_=tsp, func=AF.Ln, bias=1.0), reads=["tsp"], writes=["tsp"])
            P.add("dve", lambda e: e.tensor_scalar(out=hcr, in0=tsp, scalar1=-4.0, scalar2=None, op0=ALU.mult), reads=["tsp"], writes=["hcr"])
            P.add("dve", lambda e: e.tensor_scalar(out=hcr2, in0=tsp, scalar1=-8.0, scalar2=None, op0=ALU.mult), reads=["tsp"], writes=["hcr"])
            P.add("dve", lambda e: e.tensor_scalar(out=hbr, in0=vec[:, 60:72], scalar1=0.5, scalar2=None, op0=ALU.mult), reads=["prm"], writes=["hbr"])
            P.add("dve", lambda e: e.tensor_scalar(out=hbi, in0=vec[:, 72:84], scalar1=0.5, scalar2=None, op0=ALU.mult), reads=["prm"], writes=["hbi"])
            P.add("dve", lambda e: e.memset(hstate, 0.0), writes=["hstate%d" % c for c in range(12)])
            P.add("dve", lambda e: e.memset(halo.rearrange("p c k -> p (c k)"), 0.0), writes=["halo%d" % c for c in range(12)])
            n = 0
            for k in range(8):
                s = n % 2
                P.add("sp", lambda e, k=k, s=s: e.dma_start(out=wst[s], in_=a_w_in[l][k * 128:(k + 1) * 128, :]),
                      writes=[f"wst{s}"], is_dma=True, dma_key=f"wst{s}")
                P.add("dve" if k % 2 == 0 else "pool",
                      lambda e, k=k, s=s: e.tensor_scalar(out=Win[:, k, :], in0=wst[s], scalar1=ngt[:, k:k + 1], scalar2=1.0,
                                                          op0=ALU.mult, op1=ALU.mult),
                      reads=[f"wst{s}", "prm2"], writes=["Win"])
                n += 1
            for q in range(4):
                s = n % 2
                srcv = a_w_out[l][q * 384:(q + 1) * 384, :].rearrange("(c p) n -> p c n", p=128)
                P.add("sp", lambda e, s=s, srcv=srcv: e.dma_start(out=wst[s].rearrange("p (c n) -> p c n", c=3), in_=srcv),
                      writes=[f"wst{s}"], is_dma=True, dma_key=f"wst{s}")
                P.add("dve" if q % 2 == 0 else "pool",
                      lambda e, q=q, s=s: e.tensor_scalar(out=Wout[:, 3 * q:3 * q + 3, :].rearrange("p c n -> p (c n)"), in0=wst[s],
                                                          scalar1=0.5, scalar2=1.0, op0=ALU.mult, op1=ALU.mult),
                      reads=[f"wst{s}"], writes=["Wout"])
                n += 1
            gwf = gw.rearrange("p g c n -> p (g c n)")
            for q, (a0, a1) in enumerate([(0, 3072), (3072, 6144), (6144, 7168)]):
                s = n % 2
                P.add("sp", lambda e, s=s, a0=a0, a1=a1: e.dma_start(out=wst[s][:, 0:a1 - a0], in_=a_gw[l][:, a0:a1]),
                      writes=[f"wst{s}"], is_dma=True, dma_key=f"wst{s}")
                P.add("dve", lambda e, s=s, a0=a0, a1=a1: e.tensor_copy(gwf[:, a0:a1], wst[s][:, 0:a1 - a0]),
                      reads=[f"wst{s}"], writes=["gw"])
                n += 1

            def cw(tap, c):
                return vec[:, tap * 12 + c: tap * 12 + c + 1]

            P.barrier()

            def stage_xb(t, g):
                tp_, gp = t % 2, g % 2
                xT = xnT[tp_]
                xnT_r = [f"xnT{tp_}_{j}" for j in range(4)]
                X32, XB = xc32[gp], xcb[gp]
                for jj in range(3):
                    c = 3 * g + jj
                    xr = f"xc{gp}_{jj}"
                    for k in range(8):
                        P.add("pe", lambda e, jj=jj, c=c, k=k: e.matmul(U[jj], lhsT=Win[:, k, c * 128:(c + 1) * 128], rhs=xT[:, k, :],
                                                                      start=(k == 0), stop=(k == 7)),
                              reads=["Win"] + xnT_r, writes=[f"U{jj}"])
                    P.add("dve", lambda e, jj=jj, c=c: e.tensor_scalar(out=X32[:, jj, :], in0=U[jj], scalar1=cw(3, c),
                                                                     scalar2=vec[:, 48 + c:49 + c], op0=ALU.mult, op1=ALU.add),
                          reads=[f"U{jj}", "prm"], writes=[xr])
                for jj in range(3):
                    c = 3 * g + jj
                    xr = f"xc{gp}_{jj}"
                    for tap, sh in ((2, 1), (1, 2), (0, 3)):
                        P.add("dve", lambda e, jj=jj, c=c, tap=tap, sh=sh: e.scalar_tensor_tensor(
                            out=X32[:, jj, sh:512], in0=U[jj][:, 0:512 - sh], scalar=cw(tap, c), in1=X32[:, jj, sh:512],
                            op0=ALU.mult, op1=ALU.add), reads=[f"U{jj}", xr, "prm"], writes=[xr])
                    for tap, sh in ((2, 1), (1, 2), (0, 3)):
                        P.add("dve", lambda e, jj=jj, c=c, tap=tap, sh=sh: e.scalar_tensor_tensor(
                            out=X32[:, jj, 0:sh], in0=halo[:, c, 3 - sh:3], scalar=cw(tap, c), in1=X32[:, jj, 0:sh],
                            op0=ALU.mult, op1=ALU.add), reads=["halo%d" % c, xr, "prm"], writes=[xr])
                    P.add("dve", lambda e, jj=jj, c=c: e.tensor_copy(halo[:, c, :], U[jj][:, 509:512]),
                          reads=[f"U{jj}"], writes=["halo%d" % c])
                    P.add("pool", lambda e, jj=jj: e.tensor_copy(XB[:, jj, :], X32[:, jj, :]), reads=[xr], writes=[f"xcb{gp}_{jj}"])

            def stage_gates(t, g):
                gp = g % 2
                XB = xcb[gp]
                trb, tib = trbs[gp], tibs[gp]
                for co in range(3):
                    c = 3 * g + co
                    combos = [(i, ci) for i, (ci, cco) in enumerate(GATE_COMBOS) if cco == co]
                    for gate in range(2):
                        for n_, (i, ci) in enumerate(combos):
                            P.add("pe", lambda e, gate=gate, i=i, ci=ci, n_=n_, ncmb=len(combos): e.matmul(
                                G[gate], lhsT=gw[:, gate, g * 7 + i, :], rhs=XB[:, ci, :], start=(n_ == 0), stop=(n_ == ncmb - 1)),
                                reads=["gw", f"xcb{gp}_{ci}"], writes=[f"G{gate}"])
                        dstb = trb if gate == 0 else tib
                        hb = hbr if gate == 0 else hbi
                        P.add("act", lambda e, gate=gate, co=co, c=c, dstb=dstb, hb=hb: e.activation(
                            out=dstb[:, co, :], in_=G[gate], func=AF.Tanh, scale=0.5, bias=hb[:, c:c + 1]),
                            reads=[f"G{gate}", "hbr", "hbi"], writes=[("tr" if gate == 0 else "ti") + f"{gp}_{co}"])

            def stage_chain(t, g, hook=None):
                gp = g % 2
                X32 = xc32[gp]
                trb, tib = trbs[gp], tibs[gp]
                for co in range(3):
                    c = 3 * g + co
                    P.add("act", lambda e, co=co, c=c: e.activation(out=qb[:, co, :], in_=trb[:, co, :], func=AF.Exp,
                                                                  scale=hcr2[:, c:c + 1], bias=hcr2[:, c:c + 1]),
                          reads=[f"tr{gp}_{co}", "hcr"], writes=["qb"])
                    P.add("act", lambda e, co=co, c=c: e.activation(out=trb[:, co, :], in_=trb[:, co, :], func=AF.Exp,
                                                                  scale=hcr[:, c:c + 1], bias=hcr[:, c:c + 1]),
                          reads=[f"tr{gp}_{co}", "hcr"], writes=[f"tr{gp}_{co}"])
                P.add("act", lambda e: e.activation(out=qb, in_=qb, func=AF.Sqrt, scale=-0.25, bias=0.25), reads=["qb"], writes=["qb"])
                if hook is not None:
                    hook()
                for co in range(3):
                    c = 3 * g + co
                    P.add("dve", lambda e, co=co: e.scalar_tensor_tensor(out=tib[:, co, :], in0=tib[:, co, :], scalar=1.0, in1=X32[:, co, :],
                                                                       op0=ALU.add, op1=ALU.mult),
                          reads=[f"ti{gp}_{co}", f"xc{gp}_{co}"], writes=[f"ti{gp}_{co}"])
                    P.add("pool", lambda e, co=co: e.tensor_tensor(out=tib[:, co, :], in0=qb[:, co, :], in1=tib[:, co, :], op=ALU.mult),
                          reads=["qb", f"ti{gp}_{co}"], writes=[f"ti{gp}_{co}"])
                    P.add("dve", lambda e, co=co, c=c: e.tensor_tensor_scan(out=qb[:, co, :], data0=trb[:, co, :], data1=tib[:, co, :],
                                                                          initial=hstate[:, c:c + 1], op0=ALU.mult, op1=ALU.add),
                          reads=[f"tr{gp}_{co}", f"ti{gp}_{co}", "hstate%d" % c, "qb"], writes=[f"h{co}"])
                    P.add("dve", lambda e, co=co, c=c: e.tensor_copy(hstate[:, c:c + 1], qb[:, co, 511:512]),
                          reads=[f"h{co}"], writes=["hstate%d" % c])

            ngb = [0]

            def stage_gb(t, g):
                tp_ = t % 2
                xT = xnT[tp_]
                xnT_r = [f"xnT{tp_}_{j}" for j in range(4)]
                for jj in range(3):
                    c = 3 * g + jj
                    ob = ngb[0] % 2
                    ngb[0] += 1
                    for k in range(8):
                        P.add("pe", lambda e, ob=ob, c=c, k=k: e.matmul(O[ob], lhsT=Win[:, k, 1536 + c * 128:1536 + (c + 1) * 128],
                                                                      rhs=xT[:, k, :], start=(k == 0), stop=(k == 7)),
                              reads=["Win"] + xnT_r, writes=[f"O{ob}"])
                    P.add("act", lambda e, ob=ob, jj=jj: e.activation(out=spb[:, jj, :], in_=O[ob], func=AF.Tanh, scale=0.5),
                          reads=[f"O{ob}"], writes=[f"sp{jj}"])
                    P.add("dve", lambda e, jj=jj, ob=ob: e.scalar_tensor_tensor(out=spb[:, jj, :], in0=spb[:, jj, :], scalar=1.0, in1=O[ob],
                                                                                op0=ALU.add, op1=ALU.mult),
                          reads=[f"sp{jj}", f"O{ob}"], writes=[f"sp{jj}"])

            def stage_y(t, g):
                for jj in range(3):
                    c = 3 * g + jj
                    P.add("pool", lambda e, jj=jj, c=c: e.tensor_tensor(out=ybf[:, c, :], in0=qb[:, jj, :], in1=spb[:, jj, :], op=ALU.mult),
                          reads=[f"h{jj}", f"sp{jj}", "qb"], writes=[f"ybf{c}"])

            def stage_out(t):
                slot = t % 2
                ys = [f"ybf{c}" for c in range(12)]
                n_o = 0
                for j in range(4):
                    for hf in range(2):
                        ob = n_o % 2
                        for c in range(12):
                            P.add("pe", lambda e, j=j, hf=hf, c=c, ob=ob: e.matmul(O[ob], lhsT=ybf[:, c, j * 128:(j + 1) * 128],
                                                                                  rhs=Wout[:, c, hf * 512:(hf + 1) * 512],
                                                                                  start=(c == 0), stop=(c == 11)),
                                  reads=ys + ["Wout"], writes=[f"O{ob}"])
                        P.add("dve", lambda e, j=j, hf=hf, ob=ob: e.tensor_tensor(out=xt[slot][:, j, hf * 512:(hf + 1) * 512], in0=O[ob],
                                                                                 in1=xt[slot][:, j, hf * 512:(hf + 1) * 512], op=ALU.add),
                              reads=[f"O{ob}", f"xt{slot}"], writes=[f"xt{slot}"])
                        n_o += 1
                store_x(dst, dst_name, t)

            load_x(src, src_name, 0)
            if NT > 1:
                load_x(src, src_name, 1)
            front(src, src_name, 0)
            stage_xb(0, 0)
            for t in range(NT):
                for g in range(4):
                    if g < 3:
                        stage_xb(t, g + 1)
                    elif t + 1 < NT:
                        stage_xb(t + 1, 0)
                    stage_gates(t, g)
                    stage_gb(t, g)
                    if g == 0 and t > 0:
                        stage_out(t - 1)
                        if t + 1 < NT:
                            load_x(src, src_name, t + 1)
                    if g == 1 and t + 1 < NT:
                        frontA(src, src_name, t + 1)
                    hook = None
                    if g == 2 and t + 1 < NT:
                        hook = (lambda tt=t + 1: frontB(tt))
                    stage_chain(t, g, hook)
                    stage_y(t, g)
            stage_out(NT - 1)

        kvs = {}

        def carve_kv_store():
            off[0] = common_end
            kvs["kT"] = carve(SL, BF16, (2, SL))
            kvs["va"] = carve(NBL * 66, BF16, (NBL, 2, 66))
            kvs["end"] = off[0]

        def phase_KV(src, src_name):
            P.barrier()
            carve_kv_store()
            kT, va = kvs["kT"], kvs["va"]
            wkv = carve(1024, BF16, (8, 256))
            wst = carve(2048, F32, (8, 256))
            ngt = carve(8)
            gkt = carve(64)
            ssk = carve(2)
            rk = carve(2)
            kn32 = carve(128, F32, (2, 64))
            kdup = carve(128, BF16, (2, 2, 64))
            P.add("sp", lambda e: e.dma_start(out=ngt, in_=kv_ng), writes=["prm2"], is_dma=True, dma_key="prm2")
            P.add("sp", lambda e: e.dma_start(out=gkt, in_=gk_b), writes=["prm"], is_dma=True, dma_key="prm")
            P.add("sp", lambda e: e.dma_start(out=wst, in_=w_kv.rearrange("(k p) n -> p k n", p=128)), writes=["wst0"], is_dma=True, dma_key="wst0")
            for k in range(8):
                P.add("dve", lambda e, k=k: e.tensor_scalar(out=wkv[:, k, :], in0=wst[:, k, :], scalar1=ngt[:, k:k + 1], scalar2=1.0,
                                                            op0=ALU.mult, op1=ALU.mult), reads=["wst0", "prm2"], writes=["wkv"])
            P.add("pool", lambda e: e.memset(va.rearrange("p b k d -> p (b k d)"), 1.0), writes=["va"])
            xt2 = carve(4096, F32, (4, 1024))
            selt = carve(2)
            P.add("sp", lambda e: e.dma_start(out=selt, in_=sel_d), writes=["selt"], is_dma=True, dma_key="selt")

            def load_pos(n):
                slot = n % 2
                if n == 0:
                    load_x(src, src_name, NH - 1, slot=0)
                    return
                i = n - 1
                load_x(src, src_name, i, slot=slot)
                srcv = src[(NH + i) * 512:(NH + i + 1) * 512, :].rearrange("(j p) d -> p j d", p=128)
                P.add("sp", lambda e: e.dma_start(out=xt2, in_=srcv), reads=[(src_name, NH + i)], writes=["xt2"],
                      is_dma=True, dma_key="xt2")
                xa = xt[slot].rearrange("p j d -> p (j d)")
                xb_ = xt2.rearrange("p j d -> p (j d)")
                P.add("pool", lambda e: e.tensor_scalar(out=xb_, in0=xb_, scalar1=selt[:, 1:2], scalar2=1.0, op0=ALU.mult, op1=ALU.mult),
                      reads=["xt2", "selt"], writes=["xt2"])
                P.add("pool", lambda e: e.tensor_scalar(out=xa, in0=xa, scalar1=selt[:, 0:1], scalar2=1.0, op0=ALU.mult, op1=ALU.mult),
                      reads=[f"xt{slot}", "selt"], writes=[f"xt{slot}"])
                P.add("pool", lambda e: e.tensor_tensor(out=xa, in0=xa, in1=xb_, op=ALU.add), reads=[f"xt{slot}", "xt2"], writes=[f"xt{slot}"])
                dstv = xsel[i * 512:(i + 1) * 512, :].rearrange("(j p) d -> p j d", p=128)
                P.add("sp", lambda e: e.dma_start(out=dstv, in_=xt[slot]), reads=[f"xt{slot}"], writes=[("xsel", i)],
                      is_dma=True, dma_key=f"st{slot}")

            load_pos(0)
            load_pos(1)
            for t in range(NH + 1):
                front(src, src_name, t, use_ln=True)
                if t + 2 < NH + 1:
                    load_pos(t + 2)
                xT = xnT[t % 2]
                for j in range(4):
                    blk = t * 4 + j
                    xr = f"xnT{t % 2}_{j}"
                    for k in range(8):
                        P.add("pe", lambda e, j=j, k=k, xT=xT: e.matmul(U[0][:, 0:256], lhsT=xT[:, k, j * 128:(j + 1) * 128], rhs=wkv[:, k, :],
                                                                      start=(k == 0), stop=(k == 7)), reads=["wkv", xr], writes=["U0"])
                    P.add("act", lambda e, blk=blk: e.activation(out=va[:, blk, :, 0:64], in_=U[0][:, 128:256].rearrange("p (k d) -> p k d", k=2),
                                                               func=AF.Copy), reads=["U0"], writes=["va"])
                    P.add("act", lambda e: e.activation(out=junk[:, 0:128], in_=U[0][:, 0:128], func=AF.Square), reads=["U0"], writes=["junk"])
                    P.add("dve", lambda e: e.tensor_reduce(out=ssk, in_=junk[:, 0:128].rearrange("p (k d) -> p k d", k=2), axis=AX.X, op=ALU.add),
                          reads=["junk"], writes=["ssk"])
                    P.add("dve", lambda e: e.tensor_scalar(out=ssk, in0=ssk, scalar1=1.0 / 64, scalar2=EPS, op0=ALU.mult, op1=ALU.add),
                          reads=["ssk"], writes=["ssk"])
                    P.add("act", lambda e: e.activation(out=ssk, in_=ssk, func=AF.Ln), reads=["ssk"], writes=["ssk"])
                    P.add("act", lambda e: e.activation(out=rk, in_=ssk, func=AF.Exp, scale=-0.5), reads=["ssk"], writes=["rk"])
                    P.add("dve", lambda e: e.tensor_tensor(out=kn32, in0=U[0][:, 0:128].rearrange("p (k d) -> p k d", k=2),
                                                           in1=rk.unsqueeze(2).to_broadcast([128, 2, 64]), op=ALU.mult),
                          reads=["U0", "rk"], writes=["kn32"])
                    for cp in range(2):
                        P.add("dve", lambda e, cp=cp: e.tensor_tensor(out=kdup[:, :, cp, :], in0=kn32,
                                                                      in1=gkt.unsqueeze(1).to_broadcast([128, 2, 64]), op=ALU.mult),
                              reads=["kn32", "prm"], writes=["kdup"])
                    for kv in range(2):
                        P.add("pe", lambda e, kv=kv: e.transpose(out=TP[:, kv * 128:(kv + 1) * 128],
                                                                 in_=kdup[:, kv, :, :].rearrange("p c d -> p (c d)"), identity=ident),
                              reads=["kdup", "ident"], writes=["TP"])
                    P.add("dve", lambda e, blk=blk: e.tensor_copy(kT[:, :, blk * 128:(blk + 1) * 128],
                                                                   TP[:, 0:256].rearrange("p (k n) -> p k n", k=2)),
                          reads=["TP"], writes=["kT"])

        def layer_B(l, src, src_name, dst, dst_name):
            NT = NH
            P.barrier()
            off[0] = kvs["end"]
            kT, va = kvs["kT"], kvs["va"]
            Win = carve(8192, BF16, (8, 2048))
            Wout = carve(4096, BF16, (8, 1024))
            D = carve(4096, F32, (2, 16, 128))
            ngt = carve(8)
            gq = carve(1)
            gqs = carve(1)
            esk = carve(16)
            hmt = carve(1)
            _o = off[0]
            wst = [carve(2048) for _ in range(2)]
            off[0] = _o
            jq = [junk, carve(512, BF16)]
            ssq = [carve(16) for _ in range(2)]
            rq = [carve(16) for _ in range(2)]
            qn = [carve(512, BF16) for _ in range(2)]
            qT = [carve(512, BF16, (8, 128)) for _ in range(2)]
            spb = [carve(1024) for _ in range(2)]
            e1 = [carve(512) for _ in range(2)]
            ef = [carve(512) for _ in range(2)]
            PT = carve(1024, BF16, (2, 2, 4, 128))
            den = carve(4)
            rden = carve(4)
            t1 = carve(256, F32, (4, 64))
            ybf = carve(512, BF16)
            yT = carve(512, BF16, (8, 128))

            P.add("sp", lambda e: e.dma_start(out=ngt, in_=b_ng[l]), writes=["prm2"], is_dma=True, dma_key="prm2")
            P.add("sp", lambda e: e.dma_start(out=gq, in_=b_gq[l]), writes=["prm"], is_dma=True, dma_key="prm")
            P.add("sp", lambda e: e.dma_start(out=esk, in_=b_sink[l]), writes=["esk"], is_dma=True, dma_key="esk")
            P.add("sp", lambda e: e.dma_start(out=hmt, in_=hmask_d), writes=["hmt"], is_dma=True, dma_key="hmt")
            P.add("sp", lambda e: e.dma_start(out=D.rearrange("p a h q -> p (a h q)"), in_=dmat), writes=["D"], is_dma=True, dma_key="D")
            P.add("dve", lambda e: e.tensor_scalar(out=gqs, in0=gq, scalar1=0.125, scalar2=None, op0=ALU.mult), reads=["prm"], writes=["gqs"])
            P.add("act", lambda e: e.activation(out=esk, in_=esk, func=AF.Exp), reads=["esk"], writes=["esk"])
            n = 0
            for k in range(8):
                s = n % 2
                P.add("sp", lambda e, k=k, s=s: e.dma_start(out=wst[s], in_=b_w_in[l][k * 128:(k + 1) * 128, :]),
                      writes=[f"wst{s}"], is_dma=True, dma_key=f"wst{s}")
                P.add("dve" if k % 2 == 0 else "pool",
                      lambda e, k=k, s=s: e.tensor_scalar(out=Win[:, k, :], in0=wst[s], scalar1=ngt[:, k:k + 1], scalar2=1.0,
                                                          op0=ALU.mult, op1=ALU.mult), reads=[f"wst{s}", "prm2"], writes=["Win"])
                n += 1
            for q in range(4):
                s = n % 2
                srcv = b_w_out[l][q * 256:(q + 1) * 256, :].rearrange("(c p) n -> p c n", p=128)
                P.add("sp", lambda e, s=s, srcv=srcv: e.dma_start(out=wst[s].rearrange("p (c n) -> p c n", c=2), in_=srcv),
                      writes=[f"wst{s}"], is_dma=True, dma_key=f"wst{s}")
                P.add("dve" if q % 2 == 0 else "pool",
                      lambda e, q=q, s=s: e.tensor_copy(Wout[:, 2 * q:2 * q + 2, :].rearrange("p c n -> p (c n)"), wst[s]),
                      reads=[f"wst{s}"], writes=["Wout"])
                n += 1
            P.barrier()

            eskv = esk.rearrange("p (kv par i) -> p kv par i", kv=2, par=2, i=4)
            nef = [0]

            def q1(t, j):
                bp = j % 2
                xT = xnT[t % 2]
                xr = f"xnT{t % 2}_{j}"
                for f in range(2):
                    for k in range(8):
                        P.add("pe", lambda e, f=f, k=k: e.matmul(U[f], lhsT=xT[:, k, j * 128:(j + 1) * 128],
                                                               rhs=Win[:, k, f * 512:(f + 1) * 512], start=(k == 0), stop=(k == 7)),
                              reads=["Win", xr], writes=[f"U{f}"])
                    P.add("act", lambda e, f=f: e.activation(out=jq[bp][:, f * 512:(f + 1) * 512], in_=U[f], func=AF.Square),
                          reads=[f"U{f}"], writes=[f"jq{bp}_{f}"] + (["junk"] if bp == 0 else []))
                P.add("dve", lambda e: e.tensor_reduce(out=ssq[bp], in_=jq[bp].rearrange("p (h d) -> p h d", h=16), axis=AX.X, op=ALU.add),
                      reads=[f"jq{bp}_0", f"jq{bp}_1"] + (["junk"] if bp == 0 else []), writes=[f"ssq{bp}"])
                P.add("dve", lambda e: e.tensor_scalar(out=ssq[bp], in0=ssq[bp], scalar1=1.0 / 64, scalar2=EPS, op0=ALU.mult, op1=ALU.add),
                      reads=[f"ssq{bp}"], writes=[f"ssq{bp}"])
                P.add("act", lambda e: e.activation(out=ssq[bp], in_=ssq[bp], func=AF.Ln), reads=[f"ssq{bp}"], writes=[f"ssq{bp}"])
                P.add("act", lambda e: e.activation(out=rq[bp], in_=ssq[bp], func=AF.Exp, scale=-0.5), reads=[f"ssq{bp}"], writes=[f"rq{bp}"])
                for f in range(2):
                    P.add("dve", lambda e, f=f: e.tensor_tensor(out=qn[bp][:, f * 512:(f + 1) * 512].rearrange("p (h d) -> p h d", h=8),
                                                                in0=U[f].rearrange("p (h d) -> p h d", h=8),
                                                                in1=rq[bp][:, f * 8:(f + 1) * 8].unsqueeze(2).to_broadcast([128, 8, 64]),
                                                                op=ALU.mult), reads=[f"U{f}", f"rq{bp}"], writes=[f"qn{bp}_{f}"])

            def q2(t, j):
                bp = j % 2
                for pp in range(8):
                    P.add("pe", lambda e, pp=pp: e.transpose(out=TP[:, pp * 128:(pp + 1) * 128], in_=qn[bp][:, pp * 128:(pp + 1) * 128],
                                                             identity=ident), reads=[f"qn{bp}_0", f"qn{bp}_1", "ident"], writes=["TP"])
                P.add("dve", lambda e: e.tensor_scalar(out=qT[bp].rearrange("p a n -> p (a n)"), in0=TP, scalar1=gqs[:, 0:1], scalar2=None,
                                                       op0=ALU.mult), reads=["TP", "gqs"], writes=[f"qT{bp}"])

            def gat(t, j, f):
                bp = j % 2
                xT = xnT[t % 2]
                xr = f"xnT{t % 2}_{j}"
                for k in range(8):
                    P.add("pe", lambda e, k=k: e.matmul(U[2], lhsT=xT[:, k, j * 128:(j + 1) * 128],
                                                      rhs=Win[:, k, 1024 + f * 512:1024 + (f + 1) * 512],
                                                      start=(k == 0), stop=(k == 7)),
                          reads=["Win", xr], writes=["U2"])
                P.add("act", lambda e: e.activation(out=e1[f], in_=U[2], func=AF.Exp, scale=-1.0), reads=["U2"], writes=[f"e1{f}"])
                P.add("act", lambda e: e.activation(out=e1[f], in_=e1[f], func=AF.Ln, bias=1.0), reads=[f"e1{f}"], writes=[f"e1{f}"])
                P.add("act", lambda e: e.activation(out=e1[f], in_=e1[f], func=AF.Exp, scale=-1.0), reads=[f"e1{f}"], writes=[f"e1{f}"])
                P.add("dve", lambda e: e.tensor_tensor(out=spb[bp][:, f * 512:(f + 1) * 512], in0=U[2], in1=e1[f], op=ALU.mult),
                      reads=["U2", f"e1{f}"], writes=[f"sp{bp}_{f}"])

            def att(t, j, kv):
                bp = j % 2
                blk = 4 + t * 4 + j
                spv = spb[bp].rearrange("p (kv i par d) -> p kv i par d", kv=2, i=4, par=2, d=64)
                ybv = ybf.rearrange("p (kv i par d) -> p kv i par d", kv=2, i=4, par=2, d=64)
                kbs = [(0, blk - 1), (1, blk)]
                for par in range(2):
                    for kb, kblk in kbs:
                        gb = nef[0] % 2
                        nef[0] += 1
                        P.add("pe", lambda e, par=par, kblk=kblk, gb=gb: e.matmul(
                            G[gb], lhsT=kT[par * 64:(par + 1) * 64, kv, kblk * 128:(kblk + 1) * 128],
                            rhs=qT[bp][par * 64:(par + 1) * 64, 4 * kv:4 * kv + 4, :], start=True, stop=True),
                            reads=["kT", f"qT{bp}"], writes=[f"G{gb}"])
                        hs = (kv * 2 + par) * 4
                        P.add("dve", lambda e, gb=gb, kb=kb, hs=hs: e.tensor_tensor(
                            out=ef[gb].rearrange("p (i q) -> p i q", i=4), in0=G[gb].rearrange("p (i q) -> p i q", i=4),
                            in1=D[:, kb, hs:hs + 4, :], op=ALU.add), reads=[f"G{gb}", "D"], writes=[f"ef{gb}"])
                        if kb == 0 and t == 0 and j == 0:
                            P.add("dve", lambda e, gb=gb: e.tensor_scalar(out=ef[gb], in0=ef[gb], scalar1=hmt[:, 0:1], scalar2=None, op0=ALU.add),
                                  reads=[f"ef{gb}", "hmt"], writes=[f"ef{gb}"])
                        P.add("act", lambda e, gb=gb, kb=kb, par=par: e.activation(
                            out=PT[:, kb, par, :, :], in_=ef[gb].rearrange("p (i q) -> p i q", i=4), func=AF.Exp),
                            reads=[f"ef{gb}"], writes=[f"PT{kb}{par}"])
                for par in range(2):
                    for i in range(4):
                        for n_, (kb, kblk) in enumerate(kbs):
                            P.add("pe", lambda e, par=par, i=i, kb=kb, kblk=kblk, n_=n_, nk=len(kbs): e.matmul(
                                O[par][:, i * 65:(i + 1) * 65], lhsT=PT[:, kb, par, i, :], rhs=va[:, kblk, kv, 0:65],
                                start=(n_ == 0), stop=(n_ == nk - 1)),
                                reads=[f"PT{kb}{par}", "va"], writes=[f"O{par}"])
                    ov = O[par][:, 0:260].rearrange("p (i d) -> p i d", i=4)
                    P.add("dve", lambda e, ov=ov, par=par: e.tensor_tensor(out=den, in0=ov[:, :, 64], in1=eskv[:, kv, par, :], op=ALU.add),
                          reads=[f"O{par}", "esk"], writes=["den"])
                    P.add("dve", lambda e: e.reciprocal(out=rden, in_=den), reads=["den"], writes=["rden"])
                    P.add("dve", lambda e, par=par: e.tensor_tensor(out=t1, in0=spv[:, kv, :, par, :],
                                                                    in1=rden.unsqueeze(2).to_broadcast([128, 4, 64]), op=ALU.mult),
                          reads=[f"sp{bp}_0", f"sp{bp}_1", "rden"], writes=["t1"])
                    P.add("dve", lambda e, ov=ov, par=par: e.tensor_tensor(out=ybv[:, kv, :, par, :], in0=ov[:, :, 0:64], in1=t1, op=ALU.mult),
                          reads=[f"O{par}", "t1"], writes=["ybf"])

            def outp(t, j):
                slot = t % 2
                for c in range(8):
                    P.add("pe", lambda e, c=c: e.transpose(out=TP[:, c * 128:(c + 1) * 128], in_=ybf[:, c * 128:(c + 1) * 128], identity=ident),
                          reads=["ybf", "ident"], writes=["TP"])
                P.add("dve", lambda e: e.tensor_copy(yT.rearrange("p a n -> p (a n)"), TP), reads=["TP"], writes=["yT"])
                for hf in range(2):
                    for c in range(8):
                        P.add("pe", lambda e, hf=hf, c=c: e.matmul(O[hf], lhsT=yT[:, c, :], rhs=Wout[:, c, hf * 512:(hf + 1) * 512],
                                                                 start=(c == 0), stop=(c == 7)), reads=["yT", "Wout"], writes=[f"O{hf}"])
                    P.add("dve", lambda e, hf=hf: e.tensor_tensor(out=xt[slot][:, j, hf * 512:(hf + 1) * 512], in0=O[hf],
                                                                  in1=xt[slot][:, j, hf * 512:(hf + 1) * 512], op=ALU.add),
                          reads=[f"O{hf}", f"xt{slot}"], writes=[f"xt{slot}"])

            load_x(src, src_name, 0)
            if NT > 1:
                load_x(src, src_name, 1)
            front(src, src_name, 0, use_ln=True)
            q1(0, 0)
            q2(0, 0)
            gat(0, 0, 0)
            gat(0, 0, 1)
            for t in range(NT):
                for j in range(4):
                    if j < 3:
                        nxt = (t, j + 1)
                    elif t + 1 < NT:
                        nxt = (t + 1, 0)
                        front(src, src_name, t + 1, use_ln=True)
                    else:
                        nxt = None
                    if nxt:
                        q1(*nxt)
                    att(t, j, 0)
                    if nxt:
                        q2(*nxt)
                        gat(nxt[0], nxt[1], 0)
                    att(t, j, 1)
                    if nxt:
                        gat(nxt[0], nxt[1], 1)
                    outp(t, j)
                store_x(dst, dst_name, t)
                if t + 2 < NT:
                    load_x(src, src_name, t + 2)

        layer_A(0, x_in, "x", xs, "xs")
        layer_A(1, xs, "xs", xs, "xs")
        phase_KV(xs, "xs")
        layer_B(0, xsel, "xsel", xs2, "xs2")
        layer_B(1, xs2, "xs2", y_out, "y")
        P.emit(nc, st, final_waits=list(out_ops.values()))
    return nc


def _lay128(v, n):
    return np.ascontiguousarray(np.asarray(v, np.float32).reshape(n, 128).T)


def host_consts():
    slopes = np.exp2(-8.0 * np.arange(1, 17, dtype=np.float64) / 16)
    k = np.arange(128)[:, None]
    q = np.arange(128)[None, :]
    D = np.zeros((128, 2, 16, 128), np.float64)
    for kvh in range(2):
        for par in range(2):
            for i in range(4):
                h = 8 * kvh + 2 * i + par
                idx = (kvh * 2 + par) * 4 + i
                D[:, 0, idx, :] = np.where(k > q, -slopes[h] * (128 + q - k), -30000.0)
                D[:, 1, idx, :] = np.where(k <= q, -slopes[h] * (q - k), -30000.0)
    return D.astype(np.float32).reshape(128, 4096), np.eye(128, dtype=np.float32)


def host_layout(inp):
    m = {}
    D, ident = host_consts()
    m["dmat"] = D
    m["ident"] = ident
    for l in range(2):
        m[f"a_w_in{l}"] = np.ascontiguousarray(inp["a_w_in"][l], np.float32)
        m[f"a_w_out{l}"] = np.ascontiguousarray(inp["a_w_out"][l], np.float32)
        m[f"a_ng{l}"] = _lay128(inp["a_norm_g"][l], 8)
        vec = np.zeros((128, 96), np.float32)
        parts = [inp["a_conv_w"][l][0], inp["a_conv_w"][l][1], inp["a_conv_w"][l][2], inp["a_conv_w"][l][3],
                 inp["a_conv_b"][l], inp["a_gate_r_b"][l], inp["a_gate_i_b"][l], inp["a_lambda"][l]]
        for j, p in enumerate(parts):
            vec[:, j * 12:(j + 1) * 12] = _lay128(p, 12)
        m[f"a_vec{l}"] = vec
        gw = np.zeros((128, 2, 28, 128), np.float32)
        for gate, w in enumerate([inp["a_gate_r_w"][l], inp["a_gate_i_w"][l]]):
            w = np.asarray(w, np.float32)
            for g in range(4):
                Wg = np.zeros((384, 384), np.float32)
                for bb in range(4):
                    Wg[96 * bb:96 * bb + 96, 96 * bb:96 * bb + 96] = w[4 * g + bb]
                for i, (ci, co) in enumerate(GATE_COMBOS):
                    gw[:, gate, g * 7 + i, :] = Wg[128 * ci:128 * ci + 128, 128 * co:128 * co + 128]
        m[f"a_gw{l}"] = gw.reshape(128, 7168)
        m[f"b_w_in{l}"] = np.ascontiguousarray(inp["b_w_in"][l], np.float32)
        m[f"b_w_out{l}"] = np.ascontiguousarray(inp["b_w_out"][l], np.float32)
        m[f"b_ng{l}"] = _lay128(inp["b_norm_g"][l], 8)
        gq = np.asarray(inp["q_norm_g"][l], np.float32)
        m[f"b_gq{l}"] = np.ascontiguousarray(np.concatenate([gq, gq]).reshape(128, 1))
        sk = np.asarray(inp["sinks"][l], np.float32)
        sk_l = np.zeros(16, np.float32)
        for kvh in range(2):
            for par in range(2):
                for i in range(4):
                    sk_l[(kvh * 2 + par) * 4 + i] = sk[8 * kvh + 2 * i + par]
        m[f"b_sink{l}"] = np.ascontiguousarray(np.broadcast_to(sk_l[None, :], (128, 16)))
    m["w_kv"] = np.ascontiguousarray(inp["w_kv"], np.float32)
    m["kv_ng"] = _lay128(inp["kv_norm_g"], 8)
    m["gk_b"] = np.ascontiguousarray(np.broadcast_to(np.asarray(inp["k_norm_g"], np.float32)[None, :], (128, 64)))
    return m


_NC_CACHE = {}


def kernel(**inputs):
    x = np.asarray(inputs["x"], np.float32)
    B, S, Dm = x.shape
    NT = S // 512
    key = (NT,)
    if key not in _NC_CACHE:
        _NC_CACHE[key] = build(NT)
    nc = _NC_CACHE[key]
    base = host_layout(inputs)
    in_maps = []
    for b in range(B):
        xb = np.ascontiguousarray(x[b])
        for h in range(2):
            m = dict(base)
            m["x"] = xb
            sel = np.zeros((128, 2), np.float32)
            sel[:, h] = 1.0
            m["sel"] = sel
            m["hmask"] = np.full((128, 1), -30000.0 if h == 0 else 0.0, np.float32)
            in_maps.append(m)
    res = run_bass_kernel_spmd(nc, in_maps, core_ids=list(range(2 * B)))
    out = np.empty((B, S, Dm), np.float32)
    half = S // 2
    for c, r in enumerate(res.results):
        b, h = divmod(c, 2)
        out[b, h * half:(h + 1) * half] = np.asarray(r["y"], np.float32)
    return out
```

```python
import numpy as np
from contextlib import ExitStack
import concourse.bass as bass
import concourse.mybir as mybir
from concourse.bass_utils import run_bass_kernel_spmd

F32 = mybir.dt.float32
BF16 = mybir.dt.bfloat16
ALU = mybir.AluOpType
AF = mybir.ActivationFunctionType
AX = mybir.AxisListType

COMPUTE = ("pe", "act", "dve", "pool")
EPS = 1e-6


class Op:
    __slots__ = ("eng", "fn", "deps", "is_dma", "dma_key", "sig", "sem", "cnt", "order", "wkey")

    def __init__(self, eng, fn, is_dma=False, dma_key=None):
        self.eng = eng
        self.fn = fn
        self.deps = []
        self.is_dma = is_dma
        self.dma_key = dma_key
        self.sig = False
        self.sem = None
        self.cnt = 0
        self.order = 0
        self.wkey = None


class Prog:
    def __init__(self):
        self.queues = {"pe": [], "act": [], "dve": [], "pool": [], "sp": []}
        self.last_w = {}
        self.readers = {}
        self.all_ops = []
        self.barrier_deps = {}
        self.dma_last = {}

    def add(self, eng, fn, reads=(), writes=(), is_dma=False, dma_key=None):
        op = Op(eng, fn, is_dma, dma_key)
        op.order = len(self.all_ops)
        op.wkey = tuple(writes)
        deps = []
        for r in reads:
            w = self.last_w.get(r)
            if w is not None:
                deps.append(w)
        for r in writes:
            w = self.last_w.get(r)
            if w is not None:
                deps.append(w)
            deps.extend(self.readers.get(r, ()))
        if eng in self.barrier_deps:
            deps.extend(self.barrier_deps.pop(eng))
        seen = set()
        for d in deps:
            if id(d) in seen:
                continue
            seen.add(id(d))
            if d.eng == eng and eng == "pe" and not d.is_dma:
                continue
            op.deps.append(d)
        for r in writes:
            self.last_w[r] = op
            self.readers[r] = []
        for r in reads:
            if r not in writes:
                self.readers.setdefault(r, []).append(op)
        self.queues[eng].append(op)
        self.all_ops.append(op)
        if is_dma:
            self.dma_last[dma_key] = op
        return op

    def barrier(self):
        lasts = [q[-1] for q in self.queues.values() if q]
        lasts.extend(self.dma_last.values())
        for e in self.queues:
            self.barrier_deps[e] = list(lasts)

    def compress_pe_deps(self):
        q = self.queues["pe"]
        nxt = [None] * len(q)
        anchor = None
        for i in range(len(q) - 1, -1, -1):
            if i == len(q) - 1 or q[i + 1].wkey != q[i].wkey:
                anchor = q[i]
            nxt[i] = anchor
        pos = {id(op): i for i, op in enumerate(q)}
        for op in self.all_ops:
            new = []
            seen = set()
            for d in op.deps:
                if d.eng == "pe" and not d.is_dma:
                    a = nxt[pos[id(d)]]
                    if a is not d and a.order < op.order:
                        d = a
                if id(d) not in seen:
                    seen.add(id(d))
                    new.append(d)
            op.deps = new

    def emit(self, nc, stack, final_waits=()):
        self.compress_pe_deps()
        for op in self.all_ops:
            for d in op.deps:
                d.sig = True
        eng_sem = {e: stack.enter_context(nc.semaphore(f"s_{e}")) for e in COMPUTE}
        dma_sems, dma_cnt = {}, {}
        eng_cnt = {e: 0 for e in COMPUTE}
        for e, q in self.queues.items():
            for op in q:
                if op.is_dma:
                    k = op.dma_key
                    if k not in dma_sems:
                        dma_sems[k] = stack.enter_context(nc.semaphore(f"d_{k}"))
                        dma_cnt[k] = 0
                    dma_cnt[k] += 16
                    op.sem, op.cnt = dma_sems[k], dma_cnt[k]
                elif op.sig:
                    eng_cnt[e] += 1
                    op.sem, op.cnt = eng_sem[e], eng_cnt[e]
        block = stack.enter_context(nc.Block())

        def run_queue(engine, q):
            waited = {}
            for op in q:
                for d in op.deps:
                    key = id(d.sem)
                    if waited.get(key, 0) >= d.cnt:
                        continue
                    engine.wait_ge(d.sem, d.cnt)
                    waited[key] = d.cnt
                ins = op.fn(engine)
                if op.is_dma:
                    ins.then_inc(op.sem, 16)
                elif op.sig:
                    ins.then_inc(op.sem, 1)
            return waited

        qs = self.queues

        @block.tensor
        def _(eng):
            run_queue(eng, qs["pe"])

        @block.scalar
        def _(eng):
            run_queue(eng, qs["act"])

        @block.vector
        def _(eng):
            run_queue(eng, qs["dve"])

        @block.gpsimd
        def _(eng):
            run_queue(eng, qs["pool"])

        @block.sync
        def _(eng):
            waited = run_queue(eng, qs["sp"])
            for op in final_waits:
                if waited.get(id(op.sem), 0) < op.cnt:
                    eng.wait_ge(op.sem, op.cnt)
                    waited[id(op.sem)] = op.cnt


GATE_COMBOS = [(0, 0), (1, 0), (0, 1), (1, 1), (2, 1), (1, 2), (2, 2)]

DBG = {}


def build(NT, phases=("A0", "A1", "KV", "B0", "B1")):
    S = NT * 512
    NH = NT // 2
    NBL = NH * 4 + 4
    SL = NBL * 128
    nc = bass.Bass("TRN2", target_bir_lowering=False)

    def din(name, shape):
        return nc.dram_tensor(name, list(shape), F32, kind="ExternalInput").ap()

    x_in = din("x", [S, 1024])
    ident_d = din("ident", [128, 128])
    a_w_in = [din(f"a_w_in{l}", [1024, 3072]) for l in range(2)]
    a_w_out = [din(f"a_w_out{l}", [1536, 1024]) for l in range(2)]
    a_ng = [din(f"a_ng{l}", [128, 8]) for l in range(2)]
    a_vec = [din(f"a_vec{l}", [128, 96]) for l in range(2)]
    a_gw = [din(f"a_gw{l}", [128, 7168]) for l in range(2)]
    w_kv = din("w_kv", [1024, 256])
    kv_ng = din("kv_ng", [128, 8])
    gk_b = din("gk_b", [128, 64])
    b_w_in = [din(f"b_w_in{l}", [1024, 2048]) for l in range(2)]
    b_w_out = [din(f"b_w_out{l}", [1024, 1024]) for l in range(2)]
    b_ng = [din(f"b_ng{l}", [128, 8]) for l in range(2)]
    b_gq = [din(f"b_gq{l}", [128, 1]) for l in range(2)]
    b_sink = [din(f"b_sink{l}", [128, 16]) for l in range(2)]
    dmat = din("dmat", [128, 4096])
    sel_d = din("sel", [128, 2])
    hmask_d = din("hmask", [128, 1])
    y_out = nc.dram_tensor("y", [NH * 512, 1024], F32, kind="ExternalOutput").ap()
    xs = nc.dram_tensor("xs", [S, 1024], F32, kind="Internal").ap()
    xsel = nc.dram_tensor("xsel", [NH * 512, 1024], F32, kind="Internal").ap()
    xs2 = nc.dram_tensor("xs2", [NH * 512, 1024], F32, kind="Internal").ap()

    P = Prog()
    out_ops = {}

    with ExitStack() as st:
        ARENA = 53200
        arena = st.enter_context(nc.sbuf_tensor("arena", [128, ARENA], F32))
        off = [0]

        def carve(n_words, dtype=F32, shape=None):
            assert off[0] + n_words <= ARENA, (off[0], n_words)
            a = arena[:, off[0]:off[0] + n_words]
            off[0] += n_words
            if dtype != F32:
                a = a.bitcast(dtype)
            if shape is not None:
                names = " ".join(f"d{i}" for i in range(len(shape)))
                kw = {f"d{i}": s for i, s in enumerate(shape)}
                a = a.rearrange(f"p ({names}) -> p {names}", **kw)
            return a

        TP = st.enter_context(nc.psum_tensor("TP", [128, 1024], BF16))[:, :]
        U = [st.enter_context(nc.psum_tensor(f"U{i}", [128, 512], F32))[:, :] for i in range(3)]
        G = [st.enter_context(nc.psum_tensor(f"G{i}", [128, 512], F32))[:, :] for i in range(2)]
        O = [st.enter_context(nc.psum_tensor(f"O{i}", [128, 512], F32))[:, :] for i in range(2)]

        ident = carve(64, BF16)
        identf = carve(128)
        xt = [carve(4096, F32, (4, 1024)) for _ in range(2)]
        junk = carve(512, BF16)
        ss = [carve(4) for _ in range(2)]
        ms = [carve(4) for _ in range(2)]
        sd = [carve(4) for _ in range(2)]
        rstd = [carve(4) for _ in range(2)]
        xn = carve(1024, BF16, (2, 1024))
        xnT = [carve(2048, BF16, (8, 512)) for _ in range(2)]
        common_end = off[0]

        P.add("sp", lambda e: e.dma_start(out=identf, in_=ident_d), writes=["identf"], is_dma=True, dma_key="ident")
        P.add("dve", lambda e: e.tensor_copy(ident, identf), reads=["identf"], writes=["ident"])

        def load_x(src, src_name, t, slot=None):
            slot = t % 2 if slot is None else slot
            xts = xt[slot]
            srcv = src[t * 512:(t + 1) * 512, :].rearrange("(j p) d -> p j d", p=128)
            P.add("sp", lambda e: e.dma_start(out=xts, in_=srcv), reads=[(src_name, t)], writes=[f"xt{slot}"],
                  is_dma=True, dma_key=f"xt{slot}")

        def frontA(src, src_name, t):
            slot = t % 2
            xts = xt[slot]
            for j in range(4):
                P.add("act", lambda e, j=j: e.activation(out=junk, in_=xts[:, j, :], func=AF.Square,
                                                        accum_out=ss[slot][:, j:j + 1]),
                      reads=[f"xt{slot}"], writes=["junk", f"ss{slot}"])
            P.add("dve", lambda e: e.tensor_scalar(out=ms[slot], in0=ss[slot], scalar1=1.0 / 1024, scalar2=EPS,
                                                   op0=ALU.mult, op1=ALU.add), reads=[f"ss{slot}"], writes=[f"ms{slot}"])

        def frontB(t, use_ln=False):
            slot = t % 2
            xts = xt[slot]
            xT = xnT[slot]
            if use_ln:
                P.add("act", lambda e: e.activation(out=sd[slot], in_=ms[slot], func=AF.Ln), reads=[f"ms{slot}"], writes=[f"sd{slot}"])
                P.add("act", lambda e: e.activation(out=rstd[slot], in_=sd[slot], func=AF.Exp, scale=-0.5), reads=[f"sd{slot}"], writes=[f"rstd{slot}"])
            else:
                P.add("act", lambda e: e.activation(out=sd[slot], in_=ms[slot], func=AF.Sqrt), reads=[f"ms{slot}"], writes=[f"sd{slot}"])
                P.add("dve", lambda e: e.reciprocal(out=rstd[slot], in_=sd[slot]), reads=[f"sd{slot}"], writes=[f"rstd{slot}"])
            for j in range(4):
                P.add("pool", lambda e, j=j: e.tensor_scalar(out=xn[:, j % 2, :], in0=xts[:, j, :], scalar1=rstd[slot][:, j:j + 1],
                                                             scalar2=1.0, op0=ALU.mult, op1=ALU.mult),
                      reads=[f"xt{slot}", f"rstd{slot}"], writes=[f"xn{j % 2}"])
                for k in range(8):
                    P.add("pe", lambda e, j=j, k=k: e.transpose(out=TP[:, k * 128:(k + 1) * 128], in_=xn[:, j % 2, k * 128:(k + 1) * 128],
                                                               identity=ident), reads=[f"xn{j % 2}", "ident"], writes=["TP"])
                P.add("dve", lambda e, j=j: e.tensor_copy(xT[:, :, j * 128:(j + 1) * 128], TP.rearrange("p (k n) -> p k n", k=8)),
                      reads=["TP"], writes=[f"xnT{slot}_{j}"])

        def front(src, src_name, t, use_ln=False):
            frontA(src, src_name, t)
            frontB(t, use_ln)

        def store_x(dst, dst_name, t):
            slot = t % 2
            dstv = dst[t * 512:(t + 1) * 512, :].rearrange("(j p) d -> p j d", p=128)
            o = P.add("sp", lambda e: e.dma_start(out=dstv, in_=xt[slot]), reads=[f"xt{slot}"],
                      writes=[(dst_name, t)], is_dma=True, dma_key=f"st{slot}")
            out_ops[f"st{slot}"] = o

        def layer_A(l, src, src_name, dst, dst_name):
            P.barrier()
            off[0] = common_end
            Win = carve(12288, BF16, (8, 3072))
            Wout = carve(6144, BF16, (12, 1024))
            gw = carve(3584, BF16, (2, 28, 128))
            vec = carve(96)
            ngt = carve(8)
            tsp = carve(12)
            hcr = carve(12)
            hcr2 = carve(12)
            hbr = carve(12)
            hbi = carve(12)
            hstate = carve(12)
            halo = carve(36, F32, (12, 3))
            _o = off[0]
            wst = [carve(3072), carve(3072)]
            off[0] = _o
            xc32 = [carve(1536, F32, (3, 512)) for _ in range(2)]
            xcb = [carve(768, BF16, (3, 512)) for _ in range(2)]
            trbs = [carve(1536, F32, (3, 512)) for _ in range(2)]
            tibs = [carve(1536, F32, (3, 512)) for _ in range(2)]
            qb = carve(1536, F32, (3, 512))
            spb = carve(1536, F32, (3, 512))
            ybf = carve(3072, BF16, (12, 512))

            P.add("sp", lambda e: e.dma_start(out=vec, in_=a_vec[l]), writes=["prm"], is_dma=True, dma_key="prm")
            P.add("sp", lambda e: e.dma_start(out=ngt, in_=a_ng[l]), writes=["prm2"], is_dma=True, dma_key="prm2")
            lam = vec[:, 84:96]
            P.add("act", lambda e: e.activation(out=tsp, in_=lam, func=AF.Exp, scale=-1.0), reads=["prm"], writes=["tsp"])
            P.add("act", lambda e: e.activation(out=tsp, in_=tsp, func=AF.Ln, bias=1.0), reads=["tsp"], writes=["tsp"])
            P.add("dve", lambda e: e.tensor_scalar(out=hcr, in0=tsp, scalar1=-4.0, scalar2=None, op0=ALU.mult), reads=["tsp"], writes=["hcr"])
            P.add("dve", lambda e: e.tensor_scalar(out=hcr2, in0=tsp, scalar1=-8.0, scalar2=None, op0=ALU.mult), reads=["tsp"], writes=["hcr"])
            P.add("dve", lambda e: e.tensor_scalar(out=hbr, in0=vec[:, 60:72], scalar1=0.5, scalar2=None, op0=ALU.mult), reads=["prm"], writes=["hbr"])
            P.add("dve", lambda e: e.tensor_scalar(out=hbi, in0=vec[:, 72:84], scalar1=0.5, scalar2=None, op0=ALU.mult), reads=["prm"], writes=["hbi"])
            P.add("dve", lambda e: e.memset(hstate, 0.0), writes=["hstate%d" % c for c in range(12)])
            P.add("dve", lambda e: e.memset(halo.rearrange("p c k -> p (c k)"), 0.0), writes=["halo%d" % c for c in range(12)])
            n = 0
            for k in range(8):
                s = n % 2
                P.add("sp", lambda e, k=k, s=s: e.dma_start(out=wst[s], in_=a_w_in[l][k * 128:(k + 1) * 128, :]),
                      writes=[f"wst{s}"], is_dma=True, dma_key=f"wst{s}")
                P.add("dve" if k % 2 == 0 else "pool",
                      lambda e, k=k, s=s: e.tensor_scalar(out=Win[:, k, :], in0=wst[s], scalar1=ngt[:, k:k + 1], scalar2=1.0,
                                                          op0=ALU.mult, op1=ALU.mult),
                      reads=[f"wst{s}", "prm2"], writes=["Win"])
                n += 1
            for q in range(4):
                s = n % 2
                srcv = a_w_out[l][q * 384:(q + 1) * 384, :].rearrange("(c p) n -> p c n", p=128)
                P.add("sp", lambda e, s=s, srcv=srcv: e.dma_start(out=wst[s].rearrange("p (c n) -> p c n", c=3), in_=srcv),
                      writes=[f"wst{s}"], is_dma=True, dma_key=f"wst{s}")
                P.add("dve" if q % 2 == 0 else "pool",
                      lambda e, q=q, s=s: e.tensor_scalar(out=Wout[:, 3 * q:3 * q + 3, :].rearrange("p c n -> p (c n)"), in0=wst[s],
                                                          scalar1=0.5, scalar2=1.0, op0=ALU.mult, op1=ALU.mult),
                      reads=[f"wst{s}"], writes=["Wout"])
                n += 1
            gwf = gw.rearrange("p g c n -> p (g c n)")
            for q, (a0, a1) in enumerate([(0, 3072), (3072, 6144), (6144, 7168)]):
                s = n % 2
                P.add("sp", lambda e, s=s, a0=a0, a1=a1: e.dma_start(out=wst[s][:, 0:a1 - a0], in_=a_gw[l][:, a0:a1]),
                      writes=[f"wst{s}"], is_dma=True, dma_key=f"wst{s}")
                P.add("dve", lambda e, s=s, a0=a0, a1=a1: e.tensor_copy(gwf[:, a0:a1], wst[s][:, 0:a1 - a0]),
                      reads=[f"wst{s}"], writes=["gw"])
                n += 1

            def cw(tap, c):
                return vec[:, tap * 12 + c: tap * 12 + c + 1]

            P.barrier()

            def stage_xb(t, g):
                tp_, gp = t % 2, g % 2
                xT = xnT[tp_]
                xnT_r = [f"xnT{tp_}_{j}" for j in range(4)]
                X32, XB = xc32[gp], xcb[gp]
                for jj in range(3):
                    c = 3 * g + jj
                    xr = f"xc{gp}_{jj}"
                    for k in range(8):
                        P.add("pe", lambda e, jj=jj, c=c, k=k: e.matmul(U[jj], lhsT=Win[:, k, c * 128:(c + 1) * 128], rhs=xT[:, k, :],
                                                                      start=(k == 0), stop=(k == 7)),
                              reads=["Win"] + xnT_r, writes=[f"U{jj}"])
                    P.add("act", lambda e, jj=jj, c=c: e.activation(out=X32[:, jj, :], in_=U[jj], func=AF.Identity,
                                                                  scale=cw(3, c), bias=vec[:, 48 + c:49 + c]),
                          reads=[f"U{jj}", "prm"], writes=[xr])
                for jj in range(3):
                    c = 3 * g + jj
                    xr = f"xc{gp}_{jj}"
                    for tap, sh in ((2, 1), (1, 2), (0, 3)):
                        P.add("dve", lambda e, jj=jj, c=c, tap=tap, sh=sh: e.scalar_tensor_tensor(
                            out=X32[:, jj, sh:512], in0=U[jj][:, 0:512 - sh], scalar=cw(tap, c), in1=X32[:, jj, sh:512],
                            op0=ALU.mult, op1=ALU.add), reads=[f"U{jj}", xr, "prm"], writes=[xr])
                    for tap, sh in ((2, 1), (1, 2), (0, 3)):
                        P.add("dve", lambda e, jj=jj, c=c, tap=tap, sh=sh: e.scalar_tensor_tensor(
                            out=X32[:, jj, 0:sh], in0=halo[:, c, 3 - sh:3], scalar=cw(tap, c), in1=X32[:, jj, 0:sh],
                            op0=ALU.mult, op1=ALU.add), reads=["halo%d" % c, xr, "prm"], writes=[xr])
                    P.add("dve", lambda e, jj=jj, c=c: e.tensor_copy(halo[:, c, :], U[jj][:, 509:512]),
                          reads=[f"U{jj}"], writes=["halo%d" % c])
                    P.add("pool", lambda e, jj=jj: e.tensor_copy(XB[:, jj, :], X32[:, jj, :]), reads=[xr], writes=[f"xcb{gp}_{jj}"])

            def stage_gates(t, g):
                gp = g % 2
                XB = xcb[gp]
                trb, tib = trbs[gp], tibs[gp]
                for co in range(3):
                    c = 3 * g + co
                    combos = [(i, ci) for i, (ci, cco) in enumerate(GATE_COMBOS) if cco == co]
                    for gate in range(2):
                        for n_, (i, ci) in enumerate(combos):
                            P.add("pe", lambda e, gate=gate, i=i, ci=ci, n_=n_, ncmb=len(combos): e.matmul(
                                G[gate], lhsT=gw[:, gate, g * 7 + i, :], rhs=XB[:, ci, :], start=(n_ == 0), stop=(n_ == ncmb - 1)),
                                reads=["gw", f"xcb{gp}_{ci}"], writes=[f"G{gate}"])
                        dstb = trb if gate == 0 else tib
                        hb = hbr if gate == 0 else hbi
                        P.add("act", lambda e, gate=gate, co=co, c=c, dstb=dstb, hb=hb: e.activation(
                            out=dstb[:, co, :], in_=G[gate], func=AF.Tanh, scale=0.5, bias=hb[:, c:c + 1]),
                            reads=[f"G{gate}", "hbr", "hbi"], writes=[("tr" if gate == 0 else "ti") + f"{gp}_{co}"])

            def stage_chain(t, g, hook=None):
                gp = g % 2
                X32 = xc32[gp]
                trb, tib = trbs[gp], tibs[gp]
                for co in range(3):
                    c = 3 * g + co
                    P.add("act", lambda e, co=co, c=c: e.activation(out=qb[:, co, :], in_=trb[:, co, :], func=AF.Exp,
                                                                  scale=hcr2[:, c:c + 1], bias=hcr2[:, c:c + 1]),
                          reads=[f"tr{gp}_{co}", "hcr"], writes=["qb"])
                    P.add("act", lambda e, co=co, c=c: e.activation(out=trb[:, co, :], in_=trb[:, co, :], func=AF.Exp,
                                                                  scale=hcr[:, c:c + 1], bias=hcr[:, c:c + 1]),
                          reads=[f"tr{gp}_{co}", "hcr"], writes=[f"tr{gp}_{co}"])
                P.add("act", lambda e: e.activation(out=qb, in_=qb, func=AF.Sqrt, scale=-0.25, bias=0.25), reads=["qb"], writes=["qb"])
                if hook is not None:
                    hook()
                for co in range(3):
                    c = 3 * g + co
                    P.add("dve", lambda e, co=co: e.scalar_tensor_tensor(out=tib[:, co, :], in0=tib[:, co, :], scalar=1.0, in1=X32[:, co, :],
                                                                       op0=ALU.add, op1=ALU.mult),
                          reads=[f"ti{gp}_{co}", f"xc{gp}_{co}"], writes=[f"ti{gp}_{co}"])
                    P.add("pool", lambda e, co=co: e.tensor_tensor(out=tib[:, co, :], in0=qb[:, co, :], in1=tib[:, co, :], op=ALU.mult),
                          reads=["qb", f"ti{gp}_{co}"], writes=[f"ti{gp}_{co}"])
                    P.add("dve", lambda e, co=co, c=c: e.tensor_tensor_scan(out=qb[:, co, :], data0=trb[:, co, :], data1=tib[:, co, :],
                                                                          initial=hstate[:, c:c + 1], op0=ALU.mult, op1=ALU.add),
                          reads=[f"tr{gp}_{co}", f"ti{gp}_{co}", "hstate%d" % c, "qb"], writes=[f"h{co}"])

                for co in range(3):
                    c = 3 * g + co
                    P.add("pool", lambda e, co=co, c=c: e.tensor_copy(hstate[:, c:c + 1], qb[:, co, 511:512]),
                          reads=[f"h{co}"], writes=["hstate%d" % c])

            ngb = [0]

            def stage_gb(t, g):
                tp_ = t % 2
                xT = xnT[tp_]
                xnT_r = [f"xnT{tp_}_{j}" for j in range(4)]
                for jj in range(3):
                    c = 3 * g + jj
                    ob = ngb[0] % 2
                    ngb[0] += 1
                    for k in range(8):
                        P.add("pe", lambda e, ob=ob, c=c, k=k: e.matmul(O[ob], lhsT=Win[:, k, 1536 + c * 128:1536 + (c + 1) * 128],
                                                                      rhs=xT[:, k, :], start=(k == 0), stop=(k == 7)),
                              reads=["Win"] + xnT_r, writes=[f"O{ob}"])
                    P.add("act", lambda e, ob=ob, jj=jj: e.activation(out=spb[:, jj, :], in_=O[ob], func=AF.Tanh, scale=0.5),
                          reads=[f"O{ob}"], writes=[f"sp{jj}"])
                    P.add("dve", lambda e, jj=jj, ob=ob: e.scalar_tensor_tensor(out=spb[:, jj, :], in0=spb[:, jj, :], scalar=1.0, in1=O[ob],
                                                                                op0=ALU.add, op1=ALU.mult),
                          reads=[f"sp{jj}", f"O{ob}"], writes=[f"sp{jj}"])

            def stage_y(t, g):
                for jj in range(3):
                    c = 3 * g + jj
                    P.add("pool", lambda e, jj=jj, c=c: e.tensor_tensor(out=ybf[:, c, :], in0=qb[:, jj, :], in1=spb[:, jj, :], op=ALU.mult),
                          reads=[f"h{jj}", f"sp{jj}", "qb"], writes=[f"ybf{c}"])

            def stage_out(t):
                slot = t % 2
                ys = [f"ybf{c}" for c in range(12)]
                n_o = 0
                for j in range(4):
                    for hf in range(2):
                        ob = n_o % 2
                        for c in range(12):
                            P.add("pe", lambda e, j=j, hf=hf, c=c, ob=ob: e.matmul(O[ob], lhsT=ybf[:, c, j * 128:(j + 1) * 128],
                                                                                  rhs=Wout[:, c, hf * 512:(hf + 1) * 512],
                                                                                  start=(c == 0), stop=(c == 11)),
                                  reads=ys + ["Wout"], writes=[f"O{ob}"])
                        P.add("dve", lambda e, j=j, hf=hf, ob=ob: e.tensor_tensor(out=xt[slot][:, j, hf * 512:(hf + 1) * 512], in0=O[ob],
                                                                                 in1=xt[slot][:, j, hf * 512:(hf + 1) * 512], op=ALU.add),
                              reads=[f"O{ob}", f"xt{slot}"], writes=[f"xt{slot}"])
                        n_o += 1
                store_x(dst, dst_name, t)

            load_x(src, src_name, 0)
            if NT > 1:
                load_x(src, src_name, 1)
            front(src, src_name, 0)
            stage_xb(0, 0)
            for t in range(NT):
                for g in range(4):
                    if g < 3:
                        stage_xb(t, g + 1)
                    elif t + 1 < NT:
                        stage_xb(t + 1, 0)
                    stage_gates(t, g)
                    stage_gb(t, g)
                    if g == 0 and t > 0:
                        stage_out(t - 1)
                        if t + 1 < NT:
                            load_x(src, src_name, t + 1)
                    if g == 1 and t + 1 < NT:
                        frontA(src, src_name, t + 1)
                    hook = None
                    if g == 2 and t + 1 < NT:
                        hook = (lambda tt=t + 1: frontB(tt))
                    stage_chain(t, g, hook)
                    stage_y(t, g)
            stage_out(NT - 1)

        kvs = {}

        def carve_kv_store():
            off[0] = common_end
            kvs["kT"] = carve(SL, BF16, (2, SL))
            kvs["va"] = carve(NBL * 66, BF16, (NBL, 2, 66))
            kvs["end"] = off[0]

        def phase_KV(src, src_name):
            P.barrier()
            carve_kv_store()
            kT, va = kvs["kT"], kvs["va"]
            wkv = carve(1024, BF16, (8, 256))
            wst = carve(2048, F32, (8, 256))
            ngt = carve(8)
            gkt = carve(64)
            ssk = carve(2)
            rk = carve(2)
            kn32 = carve(128, F32, (2, 64))
            kdup = carve(128, BF16, (2, 2, 64))
            P.add("sp", lambda e: e.dma_start(out=ngt, in_=kv_ng), writes=["prm2"], is_dma=True, dma_key="prm2")
            P.add("sp", lambda e: e.dma_start(out=gkt, in_=gk_b), writes=["prm"], is_dma=True, dma_key="prm")
            P.add("sp", lambda e: e.dma_start(out=wst, in_=w_kv.rearrange("(k p) n -> p k n", p=128)), writes=["wst0"], is_dma=True, dma_key="wst0")
            for k in range(8):
                P.add("dve", lambda e, k=k: e.tensor_scalar(out=wkv[:, k, :], in0=wst[:, k, :], scalar1=ngt[:, k:k + 1], scalar2=1.0,
                                                            op0=ALU.mult, op1=ALU.mult), reads=["wst0", "prm2"], writes=["wkv"])
            P.add("pool", lambda e: e.memset(va.rearrange("p b k d -> p (b k d)"), 1.0), writes=["va"])
            xt2 = carve(4096, F32, (4, 1024))
            selt = carve(2)
            P.add("sp", lambda e: e.dma_start(out=selt, in_=sel_d), writes=["selt"], is_dma=True, dma_key="selt")

            def load_pos(n):
                slot = n % 2
                if n == 0:
                    load_x(src, src_name, NH - 1, slot=0)
                    return
                i = n - 1
                load_x(src, src_name, i, slot=slot)
                srcv = src[(NH + i) * 512:(NH + i + 1) * 512, :].rearrange("(j p) d -> p j d", p=128)
                P.add("sp", lambda e: e.dma_start(out=xt2, in_=srcv), reads=[(src_name, NH + i)], writes=["xt2"],
                      is_dma=True, dma_key="xt2")
                xa = xt[slot].rearrange("p j d -> p (j d)")
                xb_ = xt2.rearrange("p j d -> p (j d)")
                P.add("pool", lambda e: e.tensor_scalar(out=xb_, in0=xb_, scalar1=selt[:, 1:2], scalar2=1.0, op0=ALU.mult, op1=ALU.mult),
                      reads=["xt2", "selt"], writes=["xt2"])
                P.add("pool", lambda e: e.tensor_scalar(out=xa, in0=xa, scalar1=selt[:, 0:1], scalar2=1.0, op0=ALU.mult, op1=ALU.mult),
                      reads=[f"xt{slot}", "selt"], writes=[f"xt{slot}"])
                P.add("pool", lambda e: e.tensor_tensor(out=xa, in0=xa, in1=xb_, op=ALU.add), reads=[f"xt{slot}", "xt2"], writes=[f"xt{slot}"])
                dstv = xsel[i * 512:(i + 1) * 512, :].rearrange("(j p) d -> p j d", p=128)
                P.add("sp", lambda e: e.dma_start(out=dstv, in_=xt[slot]), reads=[f"xt{slot}"], writes=[("xsel", i)],
                      is_dma=True, dma_key=f"st{slot}")

            load_pos(0)
            load_pos(1)
            for t in range(NH + 1):
                front(src, src_name, t, use_ln=True)
                if t + 2 < NH + 1:
                    load_pos(t + 2)
                xT = xnT[t % 2]
                for j in range(4):
                    blk = t * 4 + j
                    xr = f"xnT{t % 2}_{j}"
                    for k in range(8):
                        P.add("pe", lambda e, j=j, k=k, xT=xT: e.matmul(U[0][:, 0:256], lhsT=xT[:, k, j * 128:(j + 1) * 128], rhs=wkv[:, k, :],
                                                                      start=(k == 0), stop=(k == 7)), reads=["wkv", xr], writes=["U0"])
                    P.add("act", lambda e, blk=blk: e.activation(out=va[:, blk, :, 0:64], in_=U[0][:, 128:256].rearrange("p (k d) -> p k d", k=2),
                                                               func=AF.Copy), reads=["U0"], writes=["va"])
                    P.add("act", lambda e: e.activation(out=junk[:, 0:128], in_=U[0][:, 0:128], func=AF.Square), reads=["U0"], writes=["junk"])
                    P.add("dve", lambda e: e.tensor_reduce(out=ssk, in_=junk[:, 0:128].rearrange("p (k d) -> p k d", k=2), axis=AX.X, op=ALU.add),
                          reads=["junk"], writes=["ssk"])
                    P.add("dve", lambda e: e.tensor_scalar(out=ssk, in0=ssk, scalar1=1.0 / 64, scalar2=EPS, op0=ALU.mult, op1=ALU.add),
                          reads=["ssk"], writes=["ssk"])
                    P.add("act", lambda e: e.activation(out=ssk, in_=ssk, func=AF.Ln), reads=["ssk"], writes=["ssk"])
                    P.add("act", lambda e: e.activation(out=rk, in_=ssk, func=AF.Exp, scale=-0.5), reads=["ssk"], writes=["rk"])
                    P.add("dve", lambda e: e.tensor_tensor(out=kn32, in0=U[0][:, 0:128].rearrange("p (k d) -> p k d", k=2),
                                                           in1=rk.unsqueeze(2).to_broadcast([128, 2, 64]), op=ALU.mult),
                          reads=["U0", "rk"], writes=["kn32"])
                    for cp in range(2):
                        P.add("dve", lambda e, cp=cp: e.tensor_tensor(out=kdup[:, :, cp, :], in0=kn32,
                                                                      in1=gkt.unsqueeze(1).to_broadcast([128, 2, 64]), op=ALU.mult),
                              reads=["kn32", "prm"], writes=["kdup"])
                    for kv in range(2):
                        P.add("pe", lambda e, kv=kv: e.transpose(out=TP[:, kv * 128:(kv + 1) * 128],
                                                                 in_=kdup[:, kv, :, :].rearrange("p c d -> p (c d)"), identity=ident),
                              reads=["kdup", "ident"], writes=["TP"])
                    P.add("dve", lambda e, blk=blk: e.tensor_copy(kT[:, :, blk * 128:(blk + 1) * 128],
                                                                   TP[:, 0:256].rearrange("p (k n) -> p k n", k=2)),
                          reads=["TP"], writes=["kT"])

        def layer_B(l, src, src_name, dst, dst_name):
            NT = NH
            P.barrier()
            off[0] = kvs["end"]
            kT, va = kvs["kT"], kvs["va"]
            Win = carve(8192, BF16, (8, 2048))
            Wout = carve(4096, BF16, (8, 1024))
            D = carve(4096, F32, (2, 16, 128))
            ngt = carve(8)
            gq = carve(1)
            gqs = carve(1)
            esk = carve(16)
            hmt = carve(1)
            _o = off[0]
            wst = [carve(2048) for _ in range(2)]
            off[0] = _o
            jq = [junk, carve(512, BF16)]
            ssq = [carve(16) for _ in range(2)]
            rq = [carve(16) for _ in range(2)]
            qn = [carve(512, BF16) for _ in range(2)]
            qT = [carve(512, BF16, (8, 128)) for _ in range(2)]
            spb = [carve(1024) for _ in range(2)]
            e1 = [carve(512) for _ in range(2)]
            ef = [carve(512) for _ in range(2)]
            PT = carve(1024, BF16, (2, 2, 4, 128))
            den = carve(4)
            rden = carve(4)
            t1 = carve(256, F32, (4, 64))
            ybf = carve(512, BF16)
            yT = carve(512, BF16, (8, 128))

            P.add("sp", lambda e: e.dma_start(out=ngt, in_=b_ng[l]), writes=["prm2"], is_dma=True, dma_key="prm2")
            P.add("sp", lambda e: e.dma_start(out=gq, in_=b_gq[l]), writes=["prm"], is_dma=True, dma_key="prm")
            P.add("sp", lambda e: e.dma_start(out=esk, in_=b_sink[l]), writes=["esk"], is_dma=True, dma_key="esk")
            P.add("sp", lambda e: e.dma_start(out=hmt, in_=hmask_d), writes=["hmt"], is_dma=True, dma_key="hmt")
            P.add("sp", lambda e: e.dma_start(out=D.rearrange("p a h q -> p (a h q)"), in_=dmat), writes=["D"], is_dma=True, dma_key="D")
            P.add("dve", lambda e: e.tensor_scalar(out=gqs, in0=gq, scalar1=0.125, scalar2=None, op0=ALU.mult), reads=["prm"], writes=["gqs"])
            P.add("act", lambda e: e.activation(out=esk, in_=esk, func=AF.Exp), reads=["esk"], writes=["esk"])
            n = 0
            for k in range(8):
                s = n % 2
                P.add("sp", lambda e, k=k, s=s: e.dma_start(out=wst[s], in_=b_w_in[l][k * 128:(k + 1) * 128, :]),
                      writes=[f"wst{s}"], is_dma=True, dma_key=f"wst{s}")
                P.add("dve" if k % 2 == 0 else "pool",
                      lambda e, k=k, s=s: e.tensor_scalar(out=Win[:, k, :], in0=wst[s], scalar1=ngt[:, k:k + 1], scalar2=1.0,
                                                          op0=ALU.mult, op1=ALU.mult), reads=[f"wst{s}", "prm2"], writes=["Win"])
                n += 1
            for q in range(4):
                s = n % 2
                srcv = b_w_out[l][q * 256:(q + 1) * 256, :].rearrange("(c p) n -> p c n", p=128)
                P.add("sp", lambda e, s=s, srcv=srcv: e.dma_start(out=wst[s].rearrange("p (c n) -> p c n", c=2), in_=srcv),
                      writes=[f"wst{s}"], is_dma=True, dma_key=f"wst{s}")
                P.add("dve" if q % 2 == 0 else "pool",
                      lambda e, q=q, s=s: e.tensor_copy(Wout[:, 2 * q:2 * q + 2, :].rearrange("p c n -> p (c n)"), wst[s]),
                      reads=[f"wst{s}"], writes=["Wout"])
                n += 1
            P.barrier()

            eskv = esk.rearrange("p (kv par i) -> p kv par i", kv=2, par=2, i=4)
            nef = [0]

            def q1(t, j):
                bp = j % 2
                xT = xnT[t % 2]
                xr = f"xnT{t % 2}_{j}"
                for f in range(2):
                    for k in range(8):
                        P.add("pe", lambda e, f=f, k=k: e.matmul(U[f], lhsT=xT[:, k, j * 128:(j + 1) * 128],
                                                               rhs=Win[:, k, f * 512:(f + 1) * 512], start=(k == 0), stop=(k == 7)),
                              reads=["Win", xr], writes=[f"U{f}"])
                    P.add("act", lambda e, f=f: e.activation(out=jq[bp][:, f * 512:(f + 1) * 512], in_=U[f], func=AF.Square),
                          reads=[f"U{f}"], writes=[f"jq{bp}_{f}"] + (["junk"] if bp == 0 else []))
                P.add("dve", lambda e: e.tensor_reduce(out=ssq[bp], in_=jq[bp].rearrange("p (h d) -> p h d", h=16), axis=AX.X, op=ALU.add),
                      reads=[f"jq{bp}_0", f"jq{bp}_1"] + (["junk"] if bp == 0 else []), writes=[f"ssq{bp}"])
                P.add("dve", lambda e: e.tensor_scalar(out=ssq[bp], in0=ssq[bp], scalar1=1.0 / 64, scalar2=EPS, op0=ALU.mult, op1=ALU.add),
                      reads=[f"ssq{bp}"], writes=[f"ssq{bp}"])
                P.add("act", lambda e: e.activation(out=ssq[bp], in_=ssq[bp], func=AF.Ln), reads=[f"ssq{bp}"], writes=[f"ssq{bp}"])
                P.add("act", lambda e: e.activation(out=rq[bp], in_=ssq[bp], func=AF.Exp, scale=-0.5), reads=[f"ssq{bp}"], writes=[f"rq{bp}"])
                for f in range(2):
                    P.add("dve", lambda e, f=f: e.tensor_tensor(out=qn[bp][:, f * 512:(f + 1) * 512].rearrange("p (h d) -> p h d", h=8),
                                                                in0=U[f].rearrange("p (h d) -> p h d", h=8),
                                                                in1=rq[bp][:, f * 8:(f + 1) * 8].unsqueeze(2).to_broadcast([128, 8, 64]),
                                                                op=ALU.mult), reads=[f"U{f}", f"rq{bp}"], writes=[f"qn{bp}_{f}"])

            def q2(t, j):
                bp = j % 2
                for pp in range(8):
                    P.add("pe", lambda e, pp=pp: e.transpose(out=TP[:, pp * 128:(pp + 1) * 128], in_=qn[bp][:, pp * 128:(pp + 1) * 128],
                                                             identity=ident), reads=[f"qn{bp}_0", f"qn{bp}_1", "ident"], writes=["TP"])
                P.add("dve", lambda e: e.tensor_scalar(out=qT[bp].rearrange("p a n -> p (a n)"), in0=TP, scalar1=gqs[:, 0:1], scalar2=None,
                                                       op0=ALU.mult), reads=["TP", "gqs"], writes=[f"qT{bp}"])

            def gat(t, j, f):
                bp = j % 2
                xT = xnT[t % 2]
                xr = f"xnT{t % 2}_{j}"
                for k in range(8):
                    P.add("pe", lambda e, k=k: e.matmul(U[2], lhsT=xT[:, k, j * 128:(j + 1) * 128],
                                                      rhs=Win[:, k, 1024 + f * 512:1024 + (f + 1) * 512],
                                                      start=(k == 0), stop=(k == 7)),
                          reads=["Win", xr], writes=["U2"])
                P.add("act", lambda e: e.activation(out=e1[f], in_=U[2], func=AF.Exp, scale=-1.0), reads=["U2"], writes=[f"e1{f}"])
                P.add("act", lambda e: e.activation(out=e1[f], in_=e1[f], func=AF.Ln, bias=1.0), reads=[f"e1{f}"], writes=[f"e1{f}"])
                P.add("act", lambda e: e.activation(out=e1[f], in_=e1[f], func=AF.Exp, scale=-1.0), reads=[f"e1{f}"], writes=[f"e1{f}"])
                P.add("dve", lambda e: e.tensor_tensor(out=spb[bp][:, f * 512:(f + 1) * 512], in0=U[2], in1=e1[f], op=ALU.mult),
                      reads=["U2", f"e1{f}"], writes=[f"sp{bp}_{f}"])

            def att(t, j, kv):
                bp = j % 2
                blk = 4 + t * 4 + j
                spv = spb[bp].rearrange("p (kv i par d) -> p kv i par d", kv=2, i=4, par=2, d=64)
                ybv = ybf.rearrange("p (kv i par d) -> p kv i par d", kv=2, i=4, par=2, d=64)
                kbs = [(0, blk - 1), (1, blk)]
                for par in range(2):
                    for kb, kblk in kbs:
                        gb = nef[0] % 2
                        nef[0] += 1
                        P.add("pe", lambda e, par=par, kblk=kblk, gb=gb: e.matmul(
                            G[gb], lhsT=kT[par * 64:(par + 1) * 64, kv, kblk * 128:(kblk + 1) * 128],
                            rhs=qT[bp][par * 64:(par + 1) * 64, 4 * kv:4 * kv + 4, :], start=True, stop=True),
                            reads=["kT", f"qT{bp}"], writes=[f"G{gb}"])
                        hs = (kv * 2 + par) * 4
                        P.add("dve", lambda e, gb=gb, kb=kb, hs=hs: e.tensor_tensor(
                            out=ef[gb].rearrange("p (i q) -> p i q", i=4), in0=G[gb].rearrange("p (i q) -> p i q", i=4),
                            in1=D[:, kb, hs:hs + 4, :], op=ALU.add), reads=[f"G{gb}", "D"], writes=[f"ef{gb}"])
                        if kb == 0 and t == 0 and j == 0:
                            P.add("dve", lambda e, gb=gb: e.tensor_scalar(out=ef[gb], in0=ef[gb], scalar1=hmt[:, 0:1], scalar2=None, op0=ALU.add),
                                  reads=[f"ef{gb}", "hmt"], writes=[f"ef{gb}"])
                        P.add("act", lambda e, gb=gb, kb=kb, par=par: e.activation(
                            out=PT[:, kb, par, :, :], in_=ef[gb].rearrange("p (i q) -> p i q", i=4), func=AF.Exp),
                            reads=[f"ef{gb}"], writes=[f"PT{kb}{par}"])
                for par in range(2):
                    for i in range(4):
                        for n_, (kb, kblk) in enumerate(kbs):
                            P.add("pe", lambda e, par=par, i=i, kb=kb, kblk=kblk, n_=n_, nk=len(kbs): e.matmul(
                                O[par][:, i * 65:(i + 1) * 65], lhsT=PT[:, kb, par, i, :], rhs=va[:, kblk, kv, 0:65],
                                start=(n_ == 0), stop=(n_ == nk - 1)),
                                reads=[f"PT{kb}{par}", "va"], writes=[f"O{par}"])
                    ov = O[par][:, 0:260].rearrange("p (i d) -> p i d", i=4)
                    P.add("dve", lambda e, ov=ov, par=par: e.tensor_tensor(out=den, in0=ov[:, :, 64], in1=eskv[:, kv, par, :], op=ALU.add),
                          reads=[f"O{par}", "esk"], writes=["den"])
                    P.add("dve", lambda e: e.reciprocal(out=rden, in_=den), reads=["den"], writes=["rden"])
                    P.add("dve", lambda e, par=par: e.tensor_tensor(out=t1, in0=spv[:, kv, :, par, :],
                                                                    in1=rden.unsqueeze(2).to_broadcast([128, 4, 64]), op=ALU.mult),
                          reads=[f"sp{bp}_0", f"sp{bp}_1", "rden"], writes=["t1"])
                    P.add("dve", lambda e, ov=ov, par=par: e.tensor_tensor(out=ybv[:, kv, :, par, :], in0=ov[:, :, 0:64], in1=t1, op=ALU.mult),
                          reads=[f"O{par}", "t1"], writes=["ybf"])

            def outp(t, j):
                slot = t % 2
                for c in range(8):
                    P.add("pe", lambda e, c=c: e.transpose(out=TP[:, c * 128:(c + 1) * 128], in_=ybf[:, c * 128:(c + 1) * 128], identity=ident),
                          reads=["ybf", "ident"], writes=["TP"])
                P.add("dve", lambda e: e.tensor_copy(yT.rearrange("p a n -> p (a n)"), TP), reads=["TP"], writes=["yT"])
                for hf in range(2):
                    for c in range(8):
                        P.add("pe", lambda e, hf=hf, c=c: e.matmul(O[hf], lhsT=yT[:, c, :], rhs=Wout[:, c, hf * 512:(hf + 1) * 512],
                                                                 start=(c == 0), stop=(c == 7)), reads=["yT", "Wout"], writes=[f"O{hf}"])
                    P.add("dve", lambda e, hf=hf: e.tensor_tensor(out=xt[slot][:, j, hf * 512:(hf + 1) * 512], in0=O[hf],
                                                                  in1=xt[slot][:, j, hf * 512:(hf + 1) * 512], op=ALU.add),
                          reads=[f"O{hf}", f"xt{slot}"], writes=[f"xt{slot}"])

            load_x(src, src_name, 0)
            if NT > 1:
                load_x(src, src_name, 1)
            front(src, src_name, 0, use_ln=True)
            q1(0, 0)
            q2(0, 0)
            gat(0, 0, 0)
            gat(0, 0, 1)
            for t in range(NT):
                for j in range(4):
                    if j < 3:
                        nxt = (t, j + 1)
                    elif t + 1 < NT:
                        nxt = (t + 1, 0)
                        front(src, src_name, t + 1, use_ln=True)
                    else:
                        nxt = None
                    if nxt:
                        q1(*nxt)
                    att(t, j, 0)
                    if nxt:
                        q2(*nxt)
                        gat(nxt[0], nxt[1], 0)
                    att(t, j, 1)
                    if nxt:
                        gat(nxt[0], nxt[1], 1)
                    outp(t, j)
                store_x(dst, dst_name, t)
                if t + 2 < NT:
                    load_x(src, src_name, t + 2)

        layer_A(0, x_in, "x", xs, "xs")
        layer_A(1, xs, "xs", xs, "xs")
        phase_KV(xs, "xs")
        layer_B(0, xsel, "xsel", xs2, "xs2")
        layer_B(1, xs2, "xs2", y_out, "y")
        P.emit(nc, st, final_waits=list(out_ops.values()))
    return nc


def _lay128(v, n):
    return np.ascontiguousarray(np.asarray(v, np.float32).reshape(n, 128).T)


def host_consts():
    slopes = np.exp2(-8.0 * np.arange(1, 17, dtype=np.float64) / 16)
    k = np.arange(128)[:, None]
    q = np.arange(128)[None, :]
    D = np.zeros((128, 2, 16, 128), np.float64)
    for kvh in range(2):
        for par in range(2):
            for i in range(4):
                h = 8 * kvh + 2 * i + par
                idx = (kvh * 2 + par) * 4 + i
                D[:, 0, idx, :] = np.where(k > q, -slopes[h] * (128 + q - k), -30000.0)
                D[:, 1, idx, :] = np.where(k <= q, -slopes[h] * (q - k), -30000.0)
    return D.astype(np.float32).reshape(128, 4096), np.eye(128, dtype=np.float32)


def host_layout(inp):
    m = {}
    D, ident = host_consts()
    m["dmat"] = D
    m["ident"] = ident
    for l in range(2):
        m[f"a_w_in{l}"] = np.ascontiguousarray(inp["a_w_in"][l], np.float32)
        m[f"a_w_out{l}"] = np.ascontiguousarray(inp["a_w_out"][l], np.float32)
        m[f"a_ng{l}"] = _lay128(inp["a_norm_g"][l], 8)
        vec = np.zeros((128, 96), np.float32)
        parts = [inp["a_conv_w"][l][0], inp["a_conv_w"][l][1], inp["a_conv_w"][l][2], inp["a_conv_w"][l][3],
                 inp["a_conv_b"][l], inp["a_gate_r_b"][l], inp["a_gate_i_b"][l], inp["a_lambda"][l]]
        for j, p in enumerate(parts):
            vec[:, j * 12:(j + 1) * 12] = _lay128(p, 12)
        m[f"a_vec{l}"] = vec
        gw = np.zeros((128, 2, 28, 128), np.float32)
        for gate, w in enumerate([inp["a_gate_r_w"][l], inp["a_gate_i_w"][l]]):
            w = np.asarray(w, np.float32)
            for g in range(4):
                Wg = np.zeros((384, 384), np.float32)
                for bb in range(4):
                    Wg[96 * bb:96 * bb + 96, 96 * bb:96 * bb + 96] = w[4 * g + bb]
                for i, (ci, co) in enumerate(GATE_COMBOS):
                    gw[:, gate, g * 7 + i, :] = Wg[128 * ci:128 * ci + 128, 128 * co:128 * co + 128]
        m[f"a_gw{l}"] = gw.reshape(128, 7168)
        m[f"b_w_in{l}"] = np.ascontiguousarray(inp["b_w_in"][l], np.float32)
        m[f"b_w_out{l}"] = np.ascontiguousarray(inp["b_w_out"][l], np.float32)
        m[f"b_ng{l}"] = _lay128(inp["b_norm_g"][l], 8)
        gq = np.asarray(inp["q_norm_g"][l], np.float32)
        m[f"b_gq{l}"] = np.ascontiguousarray(np.concatenate([gq, gq]).reshape(128, 1))
        sk = np.asarray(inp["sinks"][l], np.float32)
        sk_l = np.zeros(16, np.float32)
        for kvh in range(2):
            for par in range(2):
                for i in range(4):
                    sk_l[(kvh * 2 + par) * 4 + i] = sk[8 * kvh + 2 * i + par]
        m[f"b_sink{l}"] = np.ascontiguousarray(np.broadcast_to(sk_l[None, :], (128, 16)))
    m["w_kv"] = np.ascontiguousarray(inp["w_kv"], np.float32)
    m["kv_ng"] = _lay128(inp["kv_norm_g"], 8)
    m["gk_b"] = np.ascontiguousarray(np.broadcast_to(np.asarray(inp["k_norm_g"], np.float32)[None, :], (128, 64)))
    return m


_NC_CACHE = {}


def kernel(**inputs):
    x = np.asarray(inputs["x"], np.float32)
    B, S, Dm = x.shape
    NT = S // 512
    key = (NT,)
    if key not in _NC_CACHE:
        _NC_CACHE[key] = build(NT)
    nc = _NC_CACHE[key]
    base = host_layout(inputs)
    in_maps = []
    for b in range(B):
        xb = np.ascontiguousarray(x[b])
        for h in range(2):
            m = dict(base)
            m["x"] = xb
            sel = np.zeros((128, 2), np.float32)
            sel[:, h] = 1.0
            m["sel"] = sel
            m["hmask"] = np.full((128, 1), -30000.0 if h == 0 else 0.0, np.float32)
            in_maps.append(m)
    res = run_bass_kernel_spmd(nc, in_maps, core_ids=list(range(2 * B)))
    out = np.empty((B, S, Dm), np.float32)
    half = S // 2
    for c, r in enumerate(res.results):
        b, h = divmod(c, 2)
        out[b, h * half:(h + 1) * half] = np.asarray(r["y"], np.float32)
    return out
```
